# Optimizing a Trainium2 kernel written in Bass

```python
import math
import jax, jax.numpy as jnp
from jax import lax
import numpy as np

D_MODEL = 1024
BATCH = 8
SEQ = 2048
DEPTH = 4
DEC_BATCH = 128
DEC_SEQ = 4
PAST_LEN = 16384
PAGE_SIZE = 128

W_A = D_MODEL
N_BLK_A = 8
BLK_A = W_A // N_BLK_A
W_B = D_MODEL
CONV_A = 4
CONV_B = 3
LRU_C = 8.0
ALPHA = (2.0 * DEPTH) ** 0.25
BETA = (8.0 * DEPTH) ** -0.25
LN_EPS = 1e-6
N_IN = 2 * W_A + 4 * W_B + 2 * D_MODEL
SPLITS = (W_A, 2 * W_A, 2 * W_A + W_B, 2 * W_A + 2 * W_B, 2 * W_A + 3 * W_B,
          2 * W_A + 4 * W_B, 2 * W_A + 4 * W_B + D_MODEL)

kernel_name = "hybrid_rglru_shortconv_adaln_deepnorm_step"


def layer_norm(x):
    xf = x.astype(jnp.float32)
    mu = jnp.mean(xf, axis=-1, keepdims=True)
    var = jnp.mean(jnp.square(xf - mu), axis=-1, keepdims=True)
    return (xf - mu) * lax.rsqrt(var + LN_EPS)


def causal_conv(x, buf, w):
    K = w.shape[0]
    T = x.shape[1]
    xp = jnp.concatenate([buf.astype(x.dtype), x], axis=1)
    y = xp[:, 0:T] * w[0]
    for k in range(1, K):
        y = y + xp[:, k:k + T] * w[k]
    return y, xp[:, T:]


def rglru(xa, h0, w_r, b_r, w_i, b_i, lam):
    B, T, _ = xa.shape
    xf = xa.astype(jnp.float32)
    xb = xf.reshape(B, T, N_BLK_A, BLK_A)
    r = jax.nn.sigmoid(jnp.einsum('bthi,hij->bthj', xb, w_r.astype(jnp.float32)).reshape(B, T, W_A) + b_r)
    i = jax.nn.sigmoid(jnp.einsum('bthi,hij->bthj', xb, w_i.astype(jnp.float32)).reshape(B, T, W_A) + b_i)
    log_a = -LRU_C * r * jax.nn.softplus(-lam.astype(jnp.float32))
    a = jnp.exp(log_a)
    mult = jnp.sqrt(-jnp.expm1(2.0 * log_a))
    bterm = mult * (i * xf)
    bterm = bterm.at[:, 0].add(a[:, 0] * h0.astype(jnp.float32))

    def combine(left, right):
        a1, b1 = left
        a2, b2 = right
        return a1 * a2, a2 * b1 + b2

    _, h = lax.associative_scan(combine, (a, bterm), axis=1)
    return h, h[:, -1]


def trunk_layer(x, c, h0, buf_a, buf_b, w_c, b_c, w_in, conv_a_w, conv_a_b, w_r, b_r, w_i, b_i, lam,
                conv_b_w, w_a_out, w_b_out, w_o, ln_g, ln_b):
    dt = x.dtype
    mod = jax.nn.silu(c) @ w_c + b_c
    shift, scale, gate = jnp.split(mod, 3, axis=-1)
    h = (layer_norm(x) * (1.0 + scale[:, None].astype(jnp.float32)) + shift[:, None]).astype(dt)
    u = h @ w_in
    a_x, a_z, b_cg, b_bg, b_x, b_z, g_a, g_b = jnp.split(u, SPLITS, axis=-1)
    xa, new_buf_a = causal_conv(a_x, buf_a, conv_a_w)
    xa = xa + conv_a_b
    hs, h_last = rglru(xa, h0, w_r, b_r, w_i, b_i, lam)
    o_a = (hs.astype(dt) * jax.nn.silu(a_z)) @ w_a_out
    yb, new_buf_b = causal_conv(b_cg * b_x, buf_b, conv_b_w)
    o_b = (b_bg * yb * jax.nn.silu(b_z)) @ w_b_out
    merged = jax.nn.sigmoid(g_a) * o_a + jax.nn.sigmoid(g_b) * o_b
    out = merged @ w_o
    res = ALPHA * x + gate[:, None] * out
    x_new = (layer_norm(res) * ln_g + ln_b).astype(dt)
    return x_new, h_last.astype(dt), new_buf_a, new_buf_b


def setup_inputs(seed: int = 0) -> dict:
    key = jax.random.key(seed)
    ks = jax.random.split(key, 24)
    nrm = jax.random.normal
    f32 = jnp.float32
    a0 = jax.random.uniform(ks[12], (DEPTH, W_A), f32, 0.9, 0.999)
    return {
        "x_prompt": nrm(ks[0], (BATCH, SEQ, D_MODEL), f32),
        "x_sample": nrm(ks[1], (DEC_BATCH, DEC_SEQ, D_MODEL), f32),
        "state_rglru_h": 0.5 * nrm(ks[2], (DEPTH, DEC_BATCH, W_A), f32),
        "state_rglru_conv": nrm(ks[3], (DEPTH, DEC_BATCH, CONV_A - 1, W_A), f32),
        "state_sconv": 0.5 * nrm(ks[4], (DEPTH, DEC_BATCH, CONV_B - 1, W_B), f32),
        "c_prompt": nrm(ks[5], (BATCH, D_MODEL), f32),
        "c_sample": nrm(ks[6], (DEC_BATCH, D_MODEL), f32),
        "w_c": 0.5 * D_MODEL ** -0.5 * nrm(ks[7], (DEPTH, D_MODEL, 3 * D_MODEL), f32),
        "b_c": 0.01 * nrm(ks[8], (DEPTH, 3 * D_MODEL), f32),
        "w_in": D_MODEL ** -0.5 * nrm(ks[9], (DEPTH, D_MODEL, N_IN), f32),
        "conv_a_w": CONV_A ** -0.5 * nrm(ks[10], (DEPTH, CONV_A, W_A), f32),
        "conv_a_b": 0.01 * nrm(ks[11], (DEPTH, W_A), f32),
        "w_r": BLK_A ** -0.5 * nrm(ks[13], (DEPTH, N_BLK_A, BLK_A, BLK_A), f32),
        "b_r": 0.01 * nrm(ks[14], (DEPTH, W_A), f32),
        "w_i": BLK_A ** -0.5 * nrm(ks[15], (DEPTH, N_BLK_A, BLK_A, BLK_A), f32),
        "b_i": 0.01 * nrm(ks[16], (DEPTH, W_A), f32),
        "lru_lambda": jnp.log(a0) - jnp.log1p(-a0),
        "conv_b_w": CONV_B ** -0.5 * nrm(ks[17], (DEPTH, CONV_B, W_B), f32),
        "w_a_out": BETA * W_A ** -0.5 * nrm(ks[18], (DEPTH, W_A, D_MODEL), f32),
        "w_b_out": BETA * W_B ** -0.5 * nrm(ks[19], (DEPTH, W_B, D_MODEL), f32),
        "w_o": BETA * D_MODEL ** -0.5 * nrm(ks[20], (DEPTH, D_MODEL, D_MODEL), f32),
        "ln_g": 1.0 + 0.05 * nrm(ks[21], (DEPTH, D_MODEL), f32),
        "ln_b": 0.01 * nrm(ks[22], (DEPTH, D_MODEL), f32),
    }


def reference(x_prompt, x_sample, state_rglru_h, state_rglru_conv, state_sconv, c_prompt, c_sample,
              w_c, b_c, w_in, conv_a_w, conv_a_b, w_r, b_r, w_i, b_i, lru_lambda, conv_b_w,
              w_a_out, w_b_out, w_o, ln_g, ln_b):
    dt = x_prompt.dtype
    bp = x_prompt.shape[0]
    xp, xs = x_prompt, x_sample
    hp_list, cap_list, cbp_list = [], [], []
    hs_list, cas_list, cbs_list = [], [], []
    for l in range(DEPTH):
        params = (w_c[l], b_c[l], w_in[l], conv_a_w[l], conv_a_b[l], w_r[l], b_r[l], w_i[l], b_i[l],
                  lru_lambda[l], conv_b_w[l], w_a_out[l], w_b_out[l], w_o[l], ln_g[l], ln_b[l])
        h0p = jnp.zeros((bp, W_A), dt)
        bufa_p = jnp.zeros((bp, CONV_A - 1, W_A), dt)
        bufb_p = jnp.zeros((bp, CONV_B - 1, W_B), dt)
        xp, hp, cap, cbp = trunk_layer(xp, c_prompt, h0p, bufa_p, bufb_p, *params)
        xs, hs, cas, cbs = trunk_layer(xs, c_sample, state_rglru_h[l], state_rglru_conv[l], state_sconv[l], *params)
        hp_list.append(hp); cap_list.append(cap); cbp_list.append(cbp)
        hs_list.append(hs); cas_list.append(cas); cbs_list.append(cbs)
    h_prompt = jnp.stack(hp_list)
    conv_a_prompt = jnp.stack(cap_list)
    conv_b_prompt = jnp.stack(cbp_list)
    h_sample = jnp.stack(hs_list)
    conv_a_sample = jnp.stack(cas_list)
    conv_b_sample = jnp.stack(cbs_list)
    return (xp, xs, h_prompt, conv_a_prompt, conv_b_prompt, h_sample, conv_a_sample, conv_b_sample)
```

```python
import contextlib
import numpy as np
import concourse.bass as bass
import concourse.mybir as mybir
from concourse.bass_utils import run_bass_kernel_spmd

F32 = mybir.dt.float32
BF16 = mybir.dt.bfloat16
AF = mybir.ActivationFunctionType
ALU = mybir.AluOpType

D = 1024
NCH = 8
DEPTH = 4
NCORES = 8
SEQ = 2048
NSB = 16
ST = 4
ALPHA = (2.0 * DEPTH) ** 0.25
LN_EPS = 1e-6
UNITS_PER_LAYER = 45
TW = 1088

STAT_FP32R = True
CFG = dict(tslots=4, mulalt=True, groupwise=True, nwb=8, single_bx=False, ln_dedicated=False, lnr=4, window=100)


class Buf:
    __slots__ = ("name", "w", "rs", "const")

    def __init__(self, name):
        self.name = name
        self.w = None
        self.rs = []
        self.const = False


class Op:
    __slots__ = ("eng", "fn", "deps", "sem", "val", "inc", "cost", "lat", "tbl", "idx", "start", "finish", "dma")


ENGS = ("pe", "act", "dve", "pool", "sp")
SYNC_LAT = 0.25
TBL_SWITCH = 1.3
WINDOW = 48
SCHEDULE = True
PRIO_Q = 0.4


class Prog:
    def __init__(self, nc, stack):
        self.nc = nc
        self.stack = stack
        self.q = {e: [] for e in ENGS}
        self.esem = {e: stack.enter_context(nc.semaphore("sem_" + e)) for e in ENGS}
        self.dsem = {}
        self.dcnt = {}
        self.finals = []
        self.nops = 0

    def add(self, eng, fn, reads=(), writes=(), dma=None, cost=0.3, lat=0.0, tbl=0):
        deps = []
        for b in reads:
            if b.w is not None:
                deps.append(b.w)
        for b in writes:
            if b.w is not None:
                deps.append(b.w)
            deps.extend(b.rs)
        op = Op()
        op.eng = eng
        op.fn = fn
        op.cost = cost
        op.lat = lat
        op.tbl = tbl
        op.idx = self.nops
        self.nops += 1
        op.dma = dma
        op.start = None
        op.finish = None
        if dma is None:
            op.sem = self.esem[eng]
            op.val = None
            op.inc = 1
        else:
            if dma not in self.dsem:
                self.dsem[dma] = self.stack.enter_context(self.nc.semaphore("dsem_" + dma))
                self.dcnt[dma] = 0
            self.dcnt[dma] += 16
            op.sem = self.dsem[dma]
            op.val = self.dcnt[dma]
            op.inc = 16
        seen = set()
        dl = []
        for d in deps:
            if id(d) not in seen and d is not op:
                seen.add(id(d))
                dl.append(d)
        op.deps = dl
        for b in reads:
            if not b.const:
                b.rs.append(op)
        for b in writes:
            b.w = op
            b.rs = []
        self.q[eng].append(op)
        return op

    def schedule(self):
        pend = {e: list(self.q[e]) for e in ENGS}
        allops = sorted((o for e in ENGS for o in self.q[e]), key=lambda o: o.idx)
        bl = {}
        succ_max = {}
        for o in reversed(allops):
            b_ = o.cost + o.lat + succ_max.get(id(o), 0.0)
            bl[id(o)] = b_
            for d in o.deps:
                v = b_ + (0.0 if d.eng == o.eng else SYNC_LAT)
                if succ_max.get(id(d), 0.0) < v:
                    succ_max[id(d)] = v
        head = {e: 0 for e in ENGS}
        free = {e: 0.0 for e in ENGS}
        newq = {e: [] for e in ENGS}
        cur_tbl = 0
        remaining = sum(len(v) for v in pend.values())
        done = {e: [False] * len(pend[e]) for e in ENGS}
        while remaining:
            best = None
            for e in ENGS:
                lst = pend[e]
                n = len(lst)
                h = head[e]
                while h < n and done[e][h]:
                    h += 1
                head[e] = h
                cnt = 0
                i = h
                cand = None
                while i < n and cnt < CFG['window']:
                    if not done[e][i]:
                        cnt += 1
                        op = lst[i]
                        rdy = 0.0
                        ok = True
                        for d in op.deps:
                            if d.finish is None:
                                ok = False
                                break
                            t = d.finish if d.eng == e and d.dma is None else d.finish + SYNC_LAT
                            if t > rdy:
                                rdy = t
                        if ok:
                            st = rdy if rdy > free[e] else free[e]
                            sw = 1 if (e == "act" and op.tbl != 0 and op.tbl != cur_tbl) else 0
                            key = (round((st + (TBL_SWITCH if sw else 0.0)) / PRIO_Q), -bl[id(op)], op.idx)
                            if cand is None or key < cand[0]:
                                cand = (key, st, sw, i, op)
                    i += 1
                if cand is not None and (best is None or cand[0] < best[1][0]):
                    best = (e, cand)
            assert best is not None, "scheduler deadlock"
            e, (key, st, sw, i, op) = best
            if sw:
                st += TBL_SWITCH
                cur_tbl = op.tbl
            elif e == "act" and op.tbl != 0:
                cur_tbl = op.tbl
            op.start = st
            free[e] = st + op.cost
            op.finish = st + op.cost + op.lat
            done[e][i] = True
            newq[e].append(op)
            remaining -= 1
        self.q = newq
        return max(free.values())

    def emit(self):
        nc = self.nc
        prog = self
        for e in ENGS:
            k = 0
            for op in self.q[e]:
                if op.dma is None:
                    k += 1
                    op.val = k

        def run(ename, eng):
            waited = {}
            for op in prog.q[ename]:
                for d in op.deps:
                    if ename == "pe" and d.eng == "pe" and d.dma is None:
                        continue
                    k = id(d.sem)
                    if waited.get(k, 0) < d.val:
                        eng.wait_ge(d.sem, d.val)
                        waited[k] = d.val
                ins = op.fn(eng)
                ins.then_inc(op.sem, op.inc)
            if ename == "sp":
                for op in prog.finals:
                    k = id(op.sem)
                    if waited.get(k, 0) < op.val:
                        eng.wait_ge(op.sem, op.val)
                        waited[k] = op.val

        with nc.Block() as block:
            @block.tensor
            def _(e):
                run("pe", e)

            @block.scalar
            def _(e):
                run("act", e)

            @block.vector
            def _(e):
                run("dve", e)

            @block.gpsimd
            def _(e):
                run("pool", e)

            @block.sync
            def _(e):
                run("sp", e)


def bc_last(ap, n):
    return bass.AP(ap.tensor, ap.offset, [list(x) for x in ap.ap] + [[0, n]])


def build_program():
    nc = bass.Bass("TRN2", target_bir_lowering=False)
    with contextlib.ExitStack() as stack:
        _build(nc, stack)
    return nc


def _build(nc, stack):
    P = Prog(nc, stack)

    def dram(name, shape, kind):
        return nc.dram_tensor(name, shape, F32, kind=kind).ap()

    xp_d = dram("xp", [SEQ, D], "ExternalInput")
    xs_d = dram("xs", [NSB * ST, D], "ExternalInput")
    sst_d = dram("sst", [DEPTH, 96, D], "ExternalInput")
    cvec_d = dram("cvec", [17, D], "ExternalInput")
    prm_d = dram("prm", [64, D], "ExternalInput")
    wmain_d = dram("wmain", [DEPTH * UNITS_PER_LAYER, 128, 2048], "ExternalInput")
    wc_d = dram("wc", [DEPTH * 12, 128, 2048], "ExternalInput")
    ident_d = dram("ident", [128, 128], "ExternalInput")
    yp_d = dram("yp", [SEQ, D], "ExternalOutput")
    ys_d = dram("ys", [NSB * ST, D], "ExternalOutput")
    pst_d = dram("psto", [DEPTH, 6, D], "ExternalOutput")
    sso_d = dram("sso", [DEPTH, 96, D], "ExternalOutput")

    def sb(name, shape, dt=F32):
        return stack.enter_context(nc.sbuf_tensor(name, shape, dt))

    X = sb("X", [128, NCH, TW])
    HN = sb("HN", [128, NCH, TW], BF16)
    YA = sb("YA", [128, NCH, TW], BF16)
    YB = sb("YB", [128, NCH, TW], BF16)
    MGW = 576 if CFG["groupwise"] else TW
    MG = sb("MG", [128, NCH, MGW], BF16)
    XB = [[Buf(f"X{c}_{n}") for n in range(3)] for c in range(NCH)]
    HNB = [[Buf(f"HN{c}_{n}") for n in range(3)] for c in range(NCH)]
    YAB = [[Buf(f"YA{c}_{n}") for n in range(3)] for c in range(NCH)]
    YBB = [[Buf(f"YB{c}_{n}") for n in range(3)] for c in range(NCH)]
    MGB = [[Buf(f"MG{c}_{n}") for n in range(3)] for c in range(NCH)]

    NWB = CFG['nwb']
    WB = [sb(f"WB{i}", [128, 2048], BF16) for i in range(NWB)]
    WBB = [Buf(f"WB{i}") for i in range(NWB)]
    WG = sb("WG", [128, 2048], BF16)
    WGB = Buf("WG")

    PRM = sb("PRM", [128, NCH, 64])
    PRMB = Buf("PRM")
    SST = sb("SST", [128, DEPTH, NCH, 96])
    SSTB = [[Buf(f"SST{l}_{c}") for c in range(NCH)] for l in range(DEPTH)]
    PST = sb("PST", [128, DEPTH, NCH, 6])
    PSTB = [[Buf(f"PST{l}_{c}") for c in range(NCH)] for l in range(DEPTH)]
    MOD = sb("MOD", [128, DEPTH, 24, 17])
    MODB = [Buf(f"MOD{l}") for l in range(DEPTH)]
    SCT = sb("SCT", [128, NCH, 17])
    SCTB = Buf("SCT")
    SH = sb("SH", [128, DEPTH, NCH])
    S1 = sb("S1", [128, DEPTH, NCH])
    HBR = sb("HBR", [128, DEPTH, NCH])
    HBI = sb("HBI", [128, DEPTH, NCH])
    DERB = Buf("DER")
    IDN = sb("IDN", [128, 128])
    IDNB = Buf("IDN")
    ONES = sb("ONES", [128, 128])
    ONESB = Buf("ONES")

    nP1 = 18 if CFG["single_bx"] else 20
    NT = nP1 + (5 if CFG["ln_dedicated"] else 0)
    LN0 = nP1 if CFG["ln_dedicated"] else 0
    TMPALL = sb("TMPALL", [128, NT * 516])
    TMP = [TMPALL[:, i * 516:(i + 1) * 516] for i in range(NT)]
    TMPB = [Buf(f"TMP{i}") for i in range(NT)]
    IOB = [TMPALL[:, i * 516:i * 516 + 1024] for i in (14, 16, 18)]
    IOBB = [TMPB[14:16], TMPB[16:18], TMPB[18:20]]
    XAb = sb("XAb", [128, 512], BF16)
    LNR = sb("LNR", [128, CFG["lnr"], 512])
    LNRB = [Buf(f"LNR{i}") for i in range(CFG["lnr"])]
    F32R = mybir.dt.float32r
    XAbB = Buf("XAb")

    PS = [stack.enter_context(nc.psum_tensor(f"PS{i}", [128, 1024], F32)) for i in range(4)]
    BANKB = [Buf(f"BANK{i}") for i in range(8)]

    def bank(i):
        return PS[i // 2][:, (i % 2) * 512:(i % 2) * 512 + 512]

    state = {"pair": 0, "stg": 0, "io": 0}

    def next_pair():
        k = state["pair"]
        state["pair"] = (k + 1) % 4
        return k

    def next_stg():
        k = state["stg"]
        state["stg"] = (k + 1) % 2
        return k

    def next_io():
        k = state["io"]
        state["io"] = (k + 1) % 3
        return k

    def tiles_of(s):
        if s == 0:
            return [(0, 0, 512, "p"), (1, 512, 512, "p"), (2, 1024, 64, "s")]
        return [(0, 0, 512, "p"), (1, 512, 512, "p")]

    def groups_of(s):
        if not CFG["groupwise"]:
            return [tiles_of(s)]
        if s == 0:
            return [[(0, 0, 512, "p")], [(1, 512, 512, "p"), (2, 1024, 64, "s")]]
        return [[(0, 0, 512, "p")], [(1, 512, 512, "p")]]

    seq = []
    for u in range(12):
        seq.append((wc_d[u], False))
    for s in range(2):
        for l in range(DEPTH):
            base = l * UNITS_PER_LAYER
            pre = (s == 0 and l < DEPTH - 1)
            seq.append((wmain_d[base], True))
            for c in range(NCH):
                for i in range(3):
                    seq.append((wmain_d[base + 1 + 3 * c + i], False))
                if pre and c < 4:
                    for q_ in range(3):
                        seq.append((wc_d[(l + 1) * 12 + 3 * c + q_], False))
            for gi in range(len(groups_of(s))):
                for j in range(NCH):
                    for i in range(2):
                        seq.append((wmain_d[base + 25 + 2 * j + i], False))
                for q in range(4):
                    seq.append((wmain_d[base + 41 + q], False))
    isgate = [g for (_, g) in seq]
    ngidx = []
    cnt_ = 0
    for g_ in isgate:
        ngidx.append(cnt_)
        if not g_:
            cnt_ += 1
    ws = {"next_load": 0, "next_get": 0, "released": 0, "slot": {}}

    def ws_pump():
        while True:
            j = ws["next_load"]
            if j >= len(seq):
                break
            if not (isgate[j] or ngidx[j] < ws["released"] + NWB):
                break
            if isgate[j]:
                dst, dB, slot, tag = WG, WGB, -1, "wg"
            else:
                slot = ngidx[j] % NWB
                dst, dB, tag = WB[slot], WBB[slot], f"wb{slot}"
            srcd = seq[j][0]
            P.add("pool", lambda e, dst=dst, srcd=srcd: e.dma_start(out=dst[:, :], in_=srcd),
                  writes=[dB], dma=tag, cost=0.65, lat=5.0)
            ws["slot"][j] = slot
            ws["next_load"] += 1

    def ws_get():
        ws_pump()
        i = ws["next_get"]
        assert i in ws["slot"], (i, ws["next_load"], ws["released"])
        slot = ws["slot"].pop(i)
        ws["next_get"] += 1
        if slot < 0:
            return WG, WGB
        return WB[slot], WBB[slot]

    def ws_release(n=1):
        ws["released"] += n
        ws_pump()

    def fs(ap):
        return ap.free_size()

    TBL = {AF.Tanh: 1, AF.Exp: 1, AF.Sqrt: 2, AF.Ln: 3}

    def act(out, in_, func, reads, writes, bias=None, scale=None):
        kw = {}
        nap = 0
        if bias is not None:
            kw["bias"] = bias
            nap += 0 if isinstance(bias, float) else 1
        if scale is not None:
            kw["scale"] = scale
            nap += 0 if isinstance(scale, float) else 1
        cost = 0.18 + 0.09 * nap + fs(out) * 0.00066
        return P.add("act", lambda e: e.activation(out=out, in_=in_, func=func, **kw), reads, writes,
                     cost=cost, tbl=TBL.get(func, 0))

    def ew_cost(eng, out, two_in, stt_=False):
        if eng == "pool":
            return 0.2 + fs(out) * (0.0021 if two_in else 0.0009)
        if stt_:
            return 0.2 + fs(out) * 0.00127
        if two_in:
            return 0.1 + fs(out) * 0.00095
        return 0.1 + fs(out) * 0.00065

    def tt(eng, out, in0, in1, op, reads, writes):
        return P.add(eng, lambda e: e.tensor_tensor(out=out, in0=in0, in1=in1, op=op), reads, writes,
                     cost=ew_cost(eng, out, True))

    def ts(eng, out, in0, s1, s2, op0, op1, reads, writes):
        if op1 is None:
            return P.add(eng, lambda e: e.tensor_scalar(out=out, in0=in0, scalar1=s1, scalar2=None, op0=op0),
                         reads, writes, cost=ew_cost(eng, out, False))
        return P.add(eng, lambda e: e.tensor_scalar(out=out, in0=in0, scalar1=s1, scalar2=s2, op0=op0, op1=op1),
                     reads, writes, cost=ew_cost(eng, out, False))

    def stt(out, in0, scalar, in1, op0, op1, reads, writes):
        return P.add("dve", lambda e: e.scalar_tensor_tensor(out=out, in0=in0, scalar=scalar, in1=in1,
                                                             op0=op0, op1=op1), reads, writes,
                     cost=ew_cost("dve", out, True, True))

    def cp(eng, out, in_, reads, writes):
        if eng == "act":
            return act(out, in_, AF.Copy, reads, writes)
        return P.add(eng, lambda e: e.tensor_copy(out=out, in_=in_), reads, writes, cost=ew_cost(eng, out, False))

    def mm_group(out, pairs, reads, writes, passes=1):
        def fn(e):
            ins = None
            n = len(pairs)
            for i, (l_, r_) in enumerate(pairs):
                ins = e.matmul(out, l_, r_, start=(i == 0), stop=(i == n - 1))
            return ins
        per = max(0.065, 0.012 + fs(out) * 0.00043) * passes
        return P.add("pe", fn, reads, writes, cost=per * len(pairs), lat=0.15)

    def transpose(out, in_, ident, reads, writes):
        return P.add("pe", lambda e: e.transpose(out, in_, ident), reads, writes, cost=0.12, lat=0.15)

    P.add("sp", lambda e: e.dma_start(out=IDN[:, :], in_=ident_d), writes=[IDNB], dma="misc", cost=0.1, lat=2.5)
    P.add("pool", lambda e: e.memset(ONES[:, :], 1.0), writes=[ONESB])
    if STAT_FP32R:
        P.add("act", lambda e: e.activation(out=ONES[:, :].bitcast(F32R), in_=ONES[:, :], func=AF.Copy), [ONESB], [ONESB])
    P.add("pool", lambda e: e.memset(PST[:, :, :, :], 0.0),
          writes=[PSTB[l][c] for l in range(DEPTH) for c in range(NCH)])

    def act_copy(out, in_, reads, writes, scale=None):
        return act(out, in_, AF.Copy, reads, writes, scale=scale)

    def load_rows_T(src_ap, nrows):
        k = next_io()
        io = IOB[k]
        P.add("sp", lambda e: e.dma_start(out=io[:nrows, :], in_=src_ap), writes=IOBB[k], dma=f"io{k}", cost=0.1, lat=3.0)
        pr = next_pair()
        pst = PS[pr]
        bb = [BANKB[2 * pr], BANKB[2 * pr + 1]]
        for c in range(NCH):
            transpose(pst[:, c * 128:c * 128 + nrows], io[:nrows, c * 128:(c + 1) * 128], IDN[:nrows, :nrows],
                      reads=IOBB[k] + [IDNB], writes=bb)
        view = pst[:, :].rearrange("p (c r) -> p c r", r=128)[:, :, 0:nrows]
        return view, bb

    def prologue_a():
        view_, bb_ = load_rows_T(prm_d, 64)
        cp("dve", PRM[:, :, :], view_, bb_, [PRMB])
        for l in range(DEPTH):
            view_, bb_ = load_rows_T(sst_d[l], 96)
            if l % 2 == 0:
                cp("dve", SST[:, l, :, :], view_, bb_, [SSTB[l][c] for c in range(NCH)])
            else:
                act_copy(SST[:, l, :, :], view_, bb_, [SSTB[l][c] for c in range(NCH)])

        def prm_col(l, idx):
            return PRM[:, :, l * 16 + idx]

        for l in range(DEPTH):
            act(S1[:, l, :], prm_col(l, 7), AF.Exp, [PRMB], [DERB], scale=-1.0)
            act(S1[:, l, :], S1[:, l, :], AF.Ln, [DERB], [DERB], bias=1.0)
            ts("dve", SH[:, l, :], S1[:, l, :], -4.0, None, ALU.mult, None, [DERB], [DERB])
            ts("dve", S1[:, l, :], S1[:, l, :], -8.0, None, ALU.mult, None, [DERB], [DERB])
            ts("dve", HBR[:, l, :], prm_col(l, 5), 0.5, None, ALU.mult, None, [PRMB], [DERB])
            ts("dve", HBI[:, l, :], prm_col(l, 6), 0.5, None, ALU.mult, None, [PRMB], [DERB])

        P.add("sp", lambda e: e.dma_start(out=IOB[0][:17, :], in_=cvec_d), writes=IOBB[0], dma="io0", cost=0.1, lat=3.0)
        act(IOB[1][:17, :], IOB[0][:17, :], AF.Tanh, IOBB[0], IOBB[1], scale=0.5)
        stt(IOB[1][:17, :], IOB[1][:17, :], 1.0, IOB[0][:17, :], ALU.add, ALU.mult, IOBB[0] + IOBB[1], IOBB[1])
        pr_ = next_pair()
        for c in range(NCH):
            transpose(PS[pr_][:, c * 17:(c + 1) * 17], IOB[1][:17, c * 128:(c + 1) * 128], IDN[:17, :17],
                      reads=IOBB[1] + [IDNB], writes=[BANKB[2 * pr_], BANKB[2 * pr_ + 1]])
        cp("dve", SCT[:, :, :], PS[pr_][:, 0:136].rearrange("p (c r) -> p c r", r=17),
           [BANKB[2 * pr_], BANKB[2 * pr_ + 1]], [SCTB])

    SCTb = sb("SCTb", [128, NCH, 17], BF16)
    MTs = sb("MTs", [128, 256])
    MTsB = Buf("MTs")

    def mod_unit(l, u):
        wb, wbB = ws_get()
        pa = next_pair()
        acc = PS[pa][:17, 0:256]
        mm_group(acc, [(SCTb[:, kc, :], wb[:, kc * 256:(kc + 1) * 256]) for kc in range(NCH)],
                 reads=[SCTB, wbB], writes=[BANKB[2 * pa]])
        ws_release(1)
        act_copy(MTs[:17, :], acc, [BANKB[2 * pa]], [MTsB], scale=0.5)
        pt = next_pair()
        ptb = [BANKB[2 * pt]]
        for h in range(2):
            transpose(PS[pt][:, h * 17:(h + 1) * 17], MTs[:17, h * 128:(h + 1) * 128], IDN[:17, :17],
                      reads=[MTsB, IDNB], writes=ptb)
        j0 = 2 * u
        part, cc0 = j0 // 8, j0 % 8
        dst = MOD[:, l, j0:j0 + 2, :]
        tt("dve", dst, PS[pt][:, 0:34].rearrange("p (c r) -> p c r", r=17),
           bc_last(PRM[:, cc0:cc0 + 2, l * 16 + 13 + part], 17), ALU.add, ptb + [PRMB], [MODB[l]])
        if part == 1:
            ts("dve", dst, dst, 1.0, None, ALU.add, None, [MODB[l]], [MODB[l]])
        elif part == 2:
            ts("dve", dst, dst, 0.25 / ALPHA, None, ALU.mult, None, [MODB[l]], [MODB[l]])

    def prologue_b():
        cp("dve", SCTb[:, :, :], SCT[:, :, :], [SCTB], [SCTB])
        for u in range(12):
            mod_unit(0, u)
        for b in [PRMB, DERB, IDNB, ONESB, SCTB]:
            b.const = True

    def v3(ap, inner):
        return ap.rearrange("p (b t) -> p b t", t=inner)

    def ln_stats(n, c0, W, eps, tm, tr, tsq):
        pr = next_pair()
        b1, b2 = 2 * pr, 2 * pr + 1
        ps1 = bank(b1)[:, 0:W]
        ps2 = bank(b2)[:, 0:W]
        if STAT_FP32R:
            ones = ONES[:, :].bitcast(F32R)
            for c in range(NCH):
                if CFG['lnr'] == 4:
                    qx, qs = c % 2, 2 + (c % 2)
                else:
                    qx, qs = c % 4, 4 + (c % 4)
                xr = LNR[:, qx, 0:W].bitcast(F32R)
                sq = LNR[:, qs, 0:W].bitcast(F32R)
                cp("dve", xr, X[:, c, c0:c0 + W], [XB[c][n]], [LNRB[qx]])
                P.add("pe", lambda e, c=c, xr=xr: e.matmul(ps1, ones, xr, start=(c == 0), stop=(c == NCH - 1)),
                      reads=[LNRB[qx], ONESB], writes=[BANKB[b1]], cost=0.24, lat=0.15)
                act(sq, X[:, c, c0:c0 + W], AF.Square, [XB[c][n]], [LNRB[qs]])
                P.add("pe", lambda e, c=c, sq=sq: e.matmul(ps2, ones, sq, start=(c == 0), stop=(c == NCH - 1)),
                      reads=[LNRB[qs], ONESB], writes=[BANKB[b2]], cost=0.24, lat=0.15)
        else:
            ones = ONES[:, :]
            mm_group(ps1, [(ones, X[:, c, c0:c0 + W]) for c in range(NCH)],
                     reads=[XB[c][n] for c in range(NCH)] + [ONESB], writes=[BANKB[b1]], passes=4)
            for c in range(NCH):
                q = tsq[c % 2]
                act(TMP[q][:, 0:W], X[:, c, c0:c0 + W], AF.Square, [XB[c][n]], [TMPB[q]])
                P.add("pe", lambda e, c=c, q=q: e.matmul(ps2, ones, TMP[q][:, 0:W], start=(c == 0),
                                                         stop=(c == NCH - 1)),
                      reads=[TMPB[q], ONESB], writes=[BANKB[b2]], cost=0.95, lat=0.15)
        M = TMP[tm][:, 0:W]
        R = TMP[tr][:, 0:W]
        MM = TMP[tsq[0]][:, 0:W]
        act_copy(M, ps1, [BANKB[b1]], [TMPB[tm]], scale=1.0 / D)
        act(MM, ps1, AF.Square, [BANKB[b1]], [TMPB[tsq[0]]], scale=1.0 / D)
        stt(R, ps2, 1.0 / D, MM, ALU.mult, ALU.subtract, [BANKB[b2], TMPB[tsq[0]]], [TMPB[tr]])
        act(R, R, AF.Ln, [TMPB[tr]], [TMPB[tr]], bias=float(eps), scale=1.0)
        act(R, R, AF.Exp, [TMPB[tr]], [TMPB[tr]], scale=-0.5)
        return M, R

    def ln_in(l, n, c0, W, kind):
        o_ = LN0
        M, R = ln_stats(n, c0, W, LN_EPS, o_, o_ + 1, (o_ + 2, o_ + 2))
        for c in range(NCH):
            q = o_ + 3 + (c % CFG['tslots'])
            t = TMP[q][:, 0:W]
            tt("dve", t, X[:, c, c0:c0 + W], M, ALU.subtract, [XB[c][n], TMPB[o_]], [TMPB[q]])
            if kind == "p":
                tt("dve" if (CFG["mulalt"] and c % 2 == 1) else "pool", t, t, R, ALU.mult, [TMPB[q], TMPB[o_ + 1]], [TMPB[q]])
                act(HN[:, c, c0:c0 + W], t, AF.Identity, [TMPB[q], MODB[l]], [HNB[c][n]],
                    bias=MOD[:, l, c, 0:1], scale=MOD[:, l, 8 + c, 0:1])
            else:
                tt("dve", t, t, R, ALU.mult, [TMPB[q], TMPB[o_ + 1]], [TMPB[q]])
                tt("dve", v3(t, ST), v3(t, ST), bc_last(MOD[:, l, 8 + c, 1:17], ST), ALU.mult,
                   [TMPB[q], MODB[l]], [TMPB[q]])
                tt("dve", v3(HN[:, c, c0:c0 + W], ST), v3(t, ST), bc_last(MOD[:, l, c, 1:17], ST), ALU.add,
                   [TMPB[q], MODB[l]], [HNB[c][n]])

    def ln_out(l, n, c0, W):
        o_ = LN0
        M, R = ln_stats(n, c0, W, LN_EPS / (ALPHA * ALPHA), o_, o_ + 1, (o_ + 2, o_ + 2))
        for c in range(NCH):
            q = o_ + 3 + (c % CFG['tslots'])
            t = TMP[q][:, 0:W]
            tt("dve", t, X[:, c, c0:c0 + W], M, ALU.subtract, [XB[c][n], TMPB[o_]], [TMPB[q]])
            tt("pool" if (W == 512 and not (CFG["mulalt"] and c % 2 == 1)) else "dve", t, t, R, ALU.mult, [TMPB[q], TMPB[o_ + 1]], [TMPB[q]])
            act(X[:, c, c0:c0 + W], t, AF.Identity, [TMPB[q], PRMB], [XB[c][n]],
                bias=PRM[:, c, l * 16 + 12:l * 16 + 13], scale=PRM[:, c, l * 16 + 11:l * 16 + 12])

    def load_x(s):
        for (n, c0, W, kind) in tiles_of(s):
            if kind == "p":
                for tb in range(4):
                    k = next_io()
                    io = IOB[k]
                    r0 = s * 1024 + c0 + tb * 128
                    P.add("sp", lambda e, io=io, r0=r0: e.dma_start(out=io[:, :], in_=xp_d[r0:r0 + 128, :]),
                          writes=IOBB[k], dma=f"io{k}", cost=0.1, lat=3.5)
                    for c in range(NCH):
                        transpose(bank(c)[:, tb * 128:(tb + 1) * 128], io[:, c * 128:(c + 1) * 128], IDN[:, :],
                                  reads=IOBB[k] + [IDNB], writes=[BANKB[c]])
                for c in range(NCH):
                    if c % 2 == 0:
                        cp("dve", X[:, c, c0:c0 + W], bank(c)[:, 0:W], [BANKB[c]], [XB[c][n]])
                    else:
                        act_copy(X[:, c, c0:c0 + W], bank(c)[:, 0:W], [BANKB[c]], [XB[c][n]])
            else:
                k = next_io()
                io = IOB[k]
                P.add("sp", lambda e, io=io: e.dma_start(out=io[:64, :], in_=xs_d), writes=IOBB[k],
                      dma=f"io{k}", cost=0.1, lat=3.0)
                for c in range(NCH):
                    transpose(bank(0)[:, c * 64:(c + 1) * 64], io[:64, c * 128:(c + 1) * 128], IDN[:64, :64],
                              reads=IOBB[k] + [IDNB], writes=[BANKB[0]])
                cp("dve", X[:, :, c0:c0 + W], bank(0)[:, 0:512].rearrange("p (c r) -> p c r", r=64),
                   [BANKB[0]], [XB[c][n] for c in range(NCH)])

    def store_rows(nr, src_fn, srcbufs, dst_ap, use_act=False):
        pr = next_pair()
        pst = PS[pr]
        bb = [BANKB[2 * pr], BANKB[2 * pr + 1]]
        for c in range(NCH):
            transpose(pst[:nr, c * 128:(c + 1) * 128], src_fn(c), IDN[:, :],
                      reads=[srcbufs[c], IDNB], writes=bb)
        k = next_io()
        io = IOB[k]
        if use_act:
            act_copy(io[:nr, :], pst[:nr, :], bb, IOBB[k])
        else:
            cp("dve", io[:nr, :], pst[:nr, :], bb, IOBB[k])
        op = P.add("sp", lambda e: e.dma_start(out=dst_ap, in_=io[:nr, :]), reads=IOBB[k], dma=f"so{k}", cost=0.1, lat=3.5)
        P.finals.append(op)

    def store_x(s, n, c0, W, kind):
        if kind == "p":
            for tb in range(4):
                r0 = s * 1024 + c0 + tb * 128
                a0 = c0 + tb * 128
                store_rows(128, lambda c, a0=a0: X[:, c, a0:a0 + 128], [XB[c][n] for c in range(NCH)],
                           yp_d[r0:r0 + 128, :], use_act=(tb % 2 == 1))
        else:
            store_rows(64, lambda c: X[:, c, c0:c0 + 64], [XB[c][n] for c in range(NCH)], ys_d)

    def store_states():
        for l in range(DEPTH):
            store_rows(6, lambda c, l=l: PST[:, l, c, :], PSTB[l], pst_d[l])
            store_rows(96, lambda c, l=l: SST[:, l, c, :], SSTB[l], sso_d[l], use_act=True)

    T_AX = (0, 1)
    T_TZ = (2, 3)
    if CFG['single_bx']:
        T_BX = (4, 4)
        T_TBZ = (5, 5)
        T_XA = (6, 7)
        T_TR = (8, 9)
        T_TI = (10, 11)
        T_A = (12, 13)
        T_A2 = (14, 15)
        T_CB, T_YB1 = 16, 17
    else:
        T_BX = (4, 5)
        T_TBZ = (6, 7)
        T_XA = (8, 9)
        T_TR = (10, 11)
        T_TI = (12, 13)
        T_A = (14, 15)
        T_A2 = (16, 17)
        T_CB, T_YB1 = 18, 19

    def wcol(wb, kc, o):
        return wb[:, kc * 256 + o * 128:kc * 256 + (o + 1) * 128]

    def proj2(wb, wbB, src, srcB, n, c0, W):
        pr = next_pair()
        ids = (2 * pr, 2 * pr + 1)
        for o in range(2):
            mm_group(bank(ids[o])[:, 0:W], [(wcol(wb, kc, o), src[:, kc, c0:c0 + W]) for kc in range(NCH)],
                     reads=[wbB] + [srcB[kc][n] for kc in range(NCH)], writes=[BANKB[ids[o]]])
        return ids

    p1 = {"k": 0}

    def p1_stage1(l, c, n, c0, W, kind, wg, wgB, wbA, wbAB, wb1, wb1B, wb2, wb2B):
        def sc(idx):
            return PRM[:, c, l * 16 + idx:l * 16 + idx + 1]
        par = p1["k"] % 2
        p1["k"] += 1
        qAX, qTZ, qBX, qTBZ = T_AX[par], T_TZ[par], T_BX[par], T_TBZ[par]
        qXA, qTR, qTI, qA, qA2 = T_XA[par], T_TR[par], T_TI[par], T_A[par], T_A2[par]
        AX = TMP[qAX]
        TZ = TMP[qTZ][:, 0:W]
        BX = TMP[qBX][:, 0:W]
        TBZ = TMP[qTBZ][:, 0:W]
        XA = TMP[qXA][:, 0:W]
        TR = TMP[qTR][:, 0:W]
        TI = TMP[qTI][:, 0:W]
        A = TMP[qA][:, 0:W]
        A2 = TMP[qA2][:, 0:W]
        CB = TMP[T_CB]
        YB1 = TMP[T_YB1][:, 0:W]
        ids = proj2(wbA, wbAB, HN, HNB, n, c0, W)
        if kind == "p":
            cp("pool", AX[:, 0:3], PST[:, l, c, 1:4], [PSTB[l][c]], [TMPB[qAX]])
            act_copy(AX[:, 3:3 + W], bank(ids[0])[:, 0:W], [BANKB[ids[0]]], [TMPB[qAX]])
            cp("pool", PST[:, l, c, 1:4], AX[:, W:W + 3], [TMPB[qAX]], [PSTB[l][c]])
            sl = lambda k: AX[:, k:k + W]
            XAo = XA
        else:
            AXv = v3(AX[:, 0:112], 7)
            cp("pool", AXv[:, :, 0:3], v3(SST[:, l, c, 16:64], 3), [SSTB[l][c]], [TMPB[qAX]])
            act_copy(AXv[:, :, 3:7], v3(bank(ids[0])[:, 0:W], ST), [BANKB[ids[0]]], [TMPB[qAX]])
            cp("pool", v3(SST[:, l, c, 16:64], 3), AXv[:, :, 4:7], [TMPB[qAX]], [SSTB[l][c]])
            sl = lambda k: AXv[:, :, k:k + ST]
            XAo = v3(XA, ST)
        act(XAo, sl(3), AF.Identity, [TMPB[qAX], PRMB], [TMPB[qXA]], bias=sc(4), scale=sc(3))
        for k in (2, 1, 0):
            stt(XAo, sl(k), sc(k), XAo, ALU.mult, ALU.add, [TMPB[qAX], PRMB, TMPB[qXA]], [TMPB[qXA]])
        act_copy(XAb[:, 0:W], XA, [TMPB[qXA]], [XAbB])
        az = bank(ids[1])[:, 0:W]
        act(TZ, az, AF.Tanh, [BANKB[ids[1]]], [TMPB[qTZ]], scale=0.5)
        stt(TZ, TZ, 1.0, az, ALU.add, ALU.mult, [TMPB[qTZ], BANKB[ids[1]]], [TMPB[qTZ]])
        ids1 = proj2(wb1, wb1B, HN, HNB, n, c0, W)
        act_copy(BX, bank(ids1[1])[:, 0:W], [BANKB[ids1[1]]], [TMPB[qBX]])
        cg = bank(ids1[0])[:, 0:W]
        if kind == "p":
            cp("pool", CB[:, 0:2], PST[:, l, c, 4:6], [PSTB[l][c]], [TMPB[T_CB]])
            tt("dve", CB[:, 2:2 + W], cg, BX, ALU.mult, [BANKB[ids1[0]], TMPB[qBX]], [TMPB[T_CB]])
            cp("pool", PST[:, l, c, 4:6], CB[:, W:W + 2], [TMPB[T_CB]], [PSTB[l][c]])
            slb = lambda k: CB[:, k:k + W]
            YBo = YB1
        else:
            CBv = v3(CB[:, 0:96], 6)
            cp("pool", CBv[:, :, 0:2], v3(SST[:, l, c, 64:96], 2), [SSTB[l][c]], [TMPB[T_CB]])
            tt("dve", CBv[:, :, 2:6], v3(cg, ST), v3(BX, ST), ALU.mult, [BANKB[ids1[0]], TMPB[qBX]],
               [TMPB[T_CB]])
            cp("pool", v3(SST[:, l, c, 64:96], 2), CBv[:, :, 4:6], [TMPB[T_CB]], [SSTB[l][c]])
            slb = lambda k: CBv[:, :, k:k + ST]
            YBo = v3(YB1, ST)
        ids2 = proj2(wb2, wb2B, HN, HNB, n, c0, W)
        bz = bank(ids2[1])[:, 0:W]
        act(TBZ, bz, AF.Tanh, [BANKB[ids2[1]]], [TMPB[qTBZ]], scale=0.5)
        stt(TBZ, TBZ, 1.0, bz, ALU.add, ALU.mult, [TMPB[qTBZ], BANKB[ids2[1]]], [TMPB[qTBZ]])
        tt("dve", TBZ, bank(ids2[0])[:, 0:W], TBZ, ALU.mult, [BANKB[ids2[0]], TMPB[qTBZ]], [TMPB[qTBZ]])
        act(YBo, slb(2), AF.Identity, [TMPB[T_CB], PRMB], [TMPB[T_YB1]], scale=sc(10))
        for k in (1, 0):
            stt(YBo, slb(k), sc(8 + k), YBo, ALU.mult, ALU.add, [TMPB[T_CB], PRMB, TMPB[T_YB1]],
                [TMPB[T_YB1]])
        tt("pool", YB[:, c, c0:c0 + W], YB1, TBZ, ALU.mult, [TMPB[T_YB1], TMPB[qTBZ]], [YBB[c][n]])
        pr = next_pair()
        gid = (2 * pr, 2 * pr + 1)
        mm_group(bank(gid[0])[:, 0:W], [(wg[:, c * 256:c * 256 + 128], XAb[:, 0:W])],
                 reads=[wgB, XAbB], writes=[BANKB[gid[0]]])
        mm_group(bank(gid[1])[:, 0:W], [(wg[:, c * 256 + 128:c * 256 + 256], XAb[:, 0:W])],
                 reads=[wgB, XAbB], writes=[BANKB[gid[1]]])
        act(TR, bank(gid[0])[:, 0:W], AF.Tanh, [BANKB[gid[0]], DERB], [TMPB[qTR]],
            bias=HBR[:, l, c:c + 1], scale=0.5)
        act(TI, bank(gid[1])[:, 0:W], AF.Tanh, [BANKB[gid[1]], DERB], [TMPB[qTI]],
            bias=HBI[:, l, c:c + 1], scale=0.5)
        act(A, TR, AF.Exp, [TMPB[qTR], DERB], [TMPB[qA]], bias=SH[:, l, c:c + 1], scale=SH[:, l, c:c + 1])
        act(A2, TR, AF.Exp, [TMPB[qTR], DERB], [TMPB[qA2]], bias=S1[:, l, c:c + 1], scale=S1[:, l, c:c + 1])
        stt(TI, TI, 1.0, XA, ALU.add, ALU.mult, [TMPB[qTI], TMPB[qXA]], [TMPB[qTI]])
        return dict(l=l, c=c, n=n, c0=c0, W=W, kind=kind, par=par)

    def p1_stage2(cx):
        l, c, n, c0, W, kind, par = cx["l"], cx["c"], cx["n"], cx["c0"], cx["W"], cx["kind"], cx["par"]
        qTZ, qXA, qTR, qTI, qA, qA2 = T_TZ[par], T_XA[par], T_TR[par], T_TI[par], T_A[par], T_A2[par]
        TZ = TMP[qTZ][:, 0:W]
        TR = TMP[qTR][:, 0:W]
        TI = TMP[qTI][:, 0:W]
        A = TMP[qA][:, 0:W]
        A2 = TMP[qA2][:, 0:W]
        act(A2, A2, AF.Sqrt, [TMPB[qA2]], [TMPB[qA2]], bias=0.25, scale=-0.25)
        tt("pool", TI, TI, A2, ALU.mult, [TMPB[qTI], TMPB[qA2]], [TMPB[qTI]])
        HS = TR
        if kind == "p":
            P.add("dve", lambda e: e.tensor_tensor_scan(
                out=HS, data0=A, data1=TI, initial=PST[:, l, c, 0:1], op0=ALU.mult, op1=ALU.add),
                [TMPB[qA], TMPB[qTI], PSTB[l][c]], [TMPB[qTR]], cost=0.1 + W * 0.0021)
            cp("pool", PST[:, l, c, 0:1], HS[:, W - 1:W], [TMPB[qTR]], [PSTB[l][c]])
        else:
            Av = v3(A, ST)
            Tv = v3(TI, ST)
            h0 = SST[:, l, c, 0:16].rearrange("p (b o) -> p b o", o=1)
            tmpv = v3(TMP[qXA][:, 0:16], 1)
            tt("dve", tmpv, Av[:, :, 0:1], h0, ALU.mult, [TMPB[qA], SSTB[l][c], TMPB[qXA]], [TMPB[qXA]])
            tt("dve", Tv[:, :, 0:1], Tv[:, :, 0:1], tmpv, ALU.add, [TMPB[qTI], TMPB[qXA]], [TMPB[qTI]])
            P.add("dve", lambda e: e.memset(Av[:, :, 0:1], 0.0), [], [TMPB[qA]])
            P.add("dve", lambda e: e.tensor_tensor_scan(
                out=HS, data0=A, data1=TI, initial=0.0, op0=ALU.mult, op1=ALU.add),
                [TMPB[qA], TMPB[qTI]], [TMPB[qTR]], cost=0.1 + W * 0.0021)
            cp("pool", h0, v3(HS, ST)[:, :, ST - 1:ST], [TMPB[qTR]], [SSTB[l][c]])
        tt("pool", YA[:, c, c0:c0 + W], HS, TZ, ALU.mult, [TMPB[qTR], TMPB[qTZ]], [YAB[c][n]])

    def phase1(l, s):
        tiles = tiles_of(s)
        wg, wgB = ws_get()
        pending = []
        for c in range(NCH):
            wbA, wbAB = ws_get()
            wb1, wb1B = ws_get()
            wb2, wb2B = ws_get()
            for (n, c0, W, kind) in tiles:
                pending.append(p1_stage1(l, c, n, c0, W, kind, wg, wgB, wbA, wbAB, wb1, wb1B, wb2, wb2B))
                if len(pending) == 2:
                    for cx in pending:
                        p1_stage2(cx)
                    pending = []
            ws_release(3)
            if s == 0 and l < DEPTH - 1 and c < 4:
                for q_ in range(3):
                    mod_unit(l + 1, 3 * c + q_)
        for cx in pending:
            p1_stage2(cx)

    def phase2(l, s, tiles, gi):
        P.tag = 'p2'
        T_TAs, T_TBs = (8, 9), (8, 9)
        T_M1 = (10, 11, 12)
        kk = 0
        for j in range(NCH):
            wa, waB = ws_get()
            for (n, c0, W, kind) in tiles:
                pr = next_pair()
                ids = (2 * pr, 2 * pr + 1)
                mm_group(bank(ids[0])[:, 0:W], [(wcol(wa, kc, 0), YA[:, kc, c0:c0 + W]) for kc in range(NCH)],
                         reads=[waB] + [YAB[kc][n] for kc in range(NCH)], writes=[BANKB[ids[0]]])
                mm_group(bank(ids[1])[:, 0:W], [(wcol(wa, kc, 1), HN[:, kc, c0:c0 + W]) for kc in range(NCH)],
                         reads=[waB] + [HNB[kc][n] for kc in range(NCH)], writes=[BANKB[ids[1]]])
                T_TA = T_TAs[kk % 2]
                kk += 1
                TA = TMP[T_TA][:, 0:W]
                M1 = TMP[T_M1[n]][:, 0:W]
                act(TA, bank(ids[1])[:, 0:W], AF.Tanh, [BANKB[ids[1]]], [TMPB[T_TA]], scale=0.5)
                stt(M1, TA, 1.0, bank(ids[0])[:, 0:W], ALU.add, ALU.mult, [TMPB[T_TA], BANKB[ids[0]]],
                    [TMPB[T_M1[n]]])
            ws_release(1)
            wb_, wbB_ = ws_get()
            for (n, c0, W, kind) in tiles:
                pr = next_pair()
                ids = (2 * pr, 2 * pr + 1)
                mm_group(bank(ids[0])[:, 0:W], [(wcol(wb_, kc, 0), YB[:, kc, c0:c0 + W]) for kc in range(NCH)],
                         reads=[wbB_] + [YBB[kc][n] for kc in range(NCH)], writes=[BANKB[ids[0]]])
                mm_group(bank(ids[1])[:, 0:W], [(wcol(wb_, kc, 1), HN[:, kc, c0:c0 + W]) for kc in range(NCH)],
                         reads=[wbB_] + [HNB[kc][n] for kc in range(NCH)], writes=[BANKB[ids[1]]])
                T_TB = T_TBs[kk % 2]
                kk += 1
                TB = TMP[T_TB][:, 0:W]
                M1 = TMP[T_M1[n]][:, 0:W]
                act(TB, bank(ids[1])[:, 0:W], AF.Tanh, [BANKB[ids[1]]], [TMPB[T_TB]], scale=0.5)
                stt(TB, TB, 1.0, bank(ids[0])[:, 0:W], ALU.add, ALU.mult, [TMPB[T_TB], BANKB[ids[0]]], [TMPB[T_TB]])
                m0 = mgoff(c0, kind)
                tt("dve", MG[:, j, m0:m0 + W], M1, TB, ALU.add, [TMPB[T_M1[n]], TMPB[T_TB]], [MGB[j][mgslot(n, kind)]])
            ws_release(1)

    def phase3(l, s, tiles, gi):
        for jj in range(4):
            wo, woB = ws_get()
            for (n, c0, W, kind) in tiles:
                m0 = mgoff(c0, kind)
                pr = next_pair()
                ids = (2 * pr, 2 * pr + 1)
                for o in range(2):
                    j = 2 * jj + o
                    po = bank(ids[o])[:, 0:W]
                    mm_group(po, [(wcol(wo, kc, o), MG[:, kc, m0:m0 + W]) for kc in range(NCH)],
                             reads=[woB] + [MGB[kc][mgslot(n, kind)] for kc in range(NCH)], writes=[BANKB[ids[o]]])
                    xs_ = X[:, j, c0:c0 + W]
                    if kind == "p":
                        stt(xs_, po, MOD[:, l, 16 + j, 0:1], xs_, ALU.mult, ALU.add,
                            [BANKB[ids[o]], MODB[l], XB[j][n]], [XB[j][n]])
                    else:
                        q = 13
                        t = TMP[q][:, 0:W]
                        tt("dve", v3(t, ST), v3(po, ST), bc_last(MOD[:, l, 16 + j, 1:17], ST), ALU.mult,
                           [BANKB[ids[o]], MODB[l]], [TMPB[q]])
                        tt("dve", xs_, xs_, t, ALU.add, [TMPB[q], XB[j][n]], [XB[j][n]])
            ws_release(1)
        for (n, c0, W, kind) in tiles:
            ln_out(l, n, c0, W)
            if l < DEPTH - 1:
                ln_in(l + 1, n, c0, W, kind)
            else:
                store_x(s, n, c0, W, kind)

    def mgoff(c0, kind):
        if not CFG["groupwise"]:
            return c0
        return 0 if kind == "p" else 512

    def mgslot(n, kind):
        if not CFG["groupwise"]:
            return n
        return 0 if kind == "p" else 2

    for s in range(2):
        load_x(s)
        if s == 0:
            prologue_a()
            prologue_b()
        for (n, c0, W, kind) in tiles_of(s):
            ln_in(0, n, c0, W, kind)
        for l in range(DEPTH):
            phase1(l, s)
            for gi, g in enumerate(groups_of(s)):
                phase2(l, s, g, gi)
                phase3(l, s, g, gi)
    store_states()
    if SCHEDULE:
        est = P.schedule()
        print(f"[kernel] scheduled: est {est:.0f} us, ops={P.nops}")
    P.emit()


def _unit(W, cols):
    sub = W[:, cols]
    return np.ascontiguousarray(sub.reshape(8, 128, 256).transpose(1, 0, 2).reshape(128, 2048))


def _pack_weights(w_c, w_in, w_r, w_i, w_a_out, w_b_out, w_o):
    wmain = np.empty((DEPTH * UNITS_PER_LAYER, 128, 2048), np.float32)
    wc = np.empty((DEPTH * 12, 128, 2048), np.float32)
    ar = np.arange(128)
    for l in range(DEPTH):
        base = l * UNITS_PER_LAYER
        g = np.empty((128, 8, 256), np.float32)
        for c in range(8):
            g[:, c, 0:128] = w_r[l, c]
            g[:, c, 128:256] = w_i[l, c]
        wmain[base] = g.reshape(128, 2048)
        k = base + 1
        for c in range(8):
            cc = c * 128 + ar
            wmain[k] = _unit(w_in[l], np.concatenate([0 * 1024 + cc, 1 * 1024 + cc])); k += 1
            wmain[k] = _unit(w_in[l], np.concatenate([2 * 1024 + cc, 4 * 1024 + cc])); k += 1
            wmain[k] = _unit(w_in[l], np.concatenate([3 * 1024 + cc, 5 * 1024 + cc])); k += 1
        for j in range(8):
            jj = j * 128 + ar
            wmain[k] = np.concatenate(
                [w_a_out[l][:, jj].reshape(8, 128, 128), w_in[l][:, 6 * 1024 + jj].reshape(8, 128, 128)],
                axis=2).transpose(1, 0, 2).reshape(128, 2048); k += 1
            wmain[k] = np.concatenate(
                [w_b_out[l][:, jj].reshape(8, 128, 128), w_in[l][:, 7 * 1024 + jj].reshape(8, 128, 128)],
                axis=2).transpose(1, 0, 2).reshape(128, 2048); k += 1
        for q in range(4):
            wmain[k] = _unit(w_o[l], q * 256 + np.arange(256)); k += 1
        assert k == base + UNITS_PER_LAYER
        for u in range(12):
            wc[l * 12 + u] = _unit(w_c[l], u * 256 + np.arange(256))
    return wmain, wc


_NC_CACHE = {}


def kernel(x_prompt, x_sample, state_rglru_h, state_rglru_conv, state_sconv, c_prompt, c_sample,
           w_c, b_c, w_in, conv_a_w, conv_a_b, w_r, b_r, w_i, b_i, lru_lambda, conv_b_w,
           w_a_out, w_b_out, w_o, ln_g, ln_b):
    f = lambda a: np.ascontiguousarray(np.asarray(a, dtype=np.float32))
    x_prompt, x_sample = f(x_prompt), f(x_sample)
    state_rglru_h, state_rglru_conv, state_sconv = f(state_rglru_h), f(state_rglru_conv), f(state_sconv)
    c_prompt, c_sample = f(c_prompt), f(c_sample)
    wmain, wc = _pack_weights(f(w_c), f(w_in), f(w_r), f(w_i), f(w_a_out), f(w_b_out), f(w_o))
    prm = np.empty((DEPTH, 16, D), np.float32)
    prm[:, 0:4] = f(conv_a_w)
    prm[:, 4] = f(conv_a_b)
    prm[:, 5] = f(b_r)
    prm[:, 6] = f(b_i)
    prm[:, 7] = f(lru_lambda)
    prm[:, 8:11] = f(conv_b_w)
    prm[:, 11] = f(ln_g)
    prm[:, 12] = f(ln_b)
    prm[:, 13:16] = f(b_c).reshape(DEPTH, 3, D)
    prm = prm.reshape(64, D)
    ident = np.eye(128, dtype=np.float32)

    if "nc" not in _NC_CACHE:
        _NC_CACHE["nc"] = build_program()
    nc = _NC_CACHE["nc"]

    in_maps = []
    for i in range(NCORES):
        bs = slice(i * NSB, (i + 1) * NSB)
        sst = np.concatenate([
            state_rglru_h[:, bs],
            state_rglru_conv[:, bs].reshape(DEPTH, NSB * 3, D),
            state_sconv[:, bs].reshape(DEPTH, NSB * 2, D),
        ], axis=1)
        in_maps.append({
            "xp": x_prompt[i],
            "xs": x_sample[bs].reshape(NSB * ST, D),
            "sst": np.ascontiguousarray(sst),
            "cvec": np.ascontiguousarray(np.concatenate([c_prompt[i:i + 1], c_sample[bs]], axis=0)),
            "prm": prm,
            "wmain": wmain,
            "wc": wc,
            "ident": ident,
        })
    res = run_bass_kernel_spmd(nc, in_maps, core_ids=list(range(NCORES)))
    R = res.results
    y_prompt = np.stack([R[i]["yp"] for i in range(NCORES)], axis=0)
    y_sample = np.concatenate([R[i]["ys"].reshape(NSB, ST, D) for i in range(NCORES)], axis=0)
    h_prompt = np.stack([R[i]["psto"][:, 0] for i in range(NCORES)], axis=1)
    conv_a_prompt = np.stack([R[i]["psto"][:, 1:4] for i in range(NCORES)], axis=1)
    conv_b_prompt = np.stack([R[i]["psto"][:, 4:6] for i in range(NCORES)], axis=1)
    h_sample = np.concatenate([R[i]["sso"][:, 0:16] for i in range(NCORES)], axis=1)
    conv_a_sample = np.concatenate([R[i]["sso"][:, 16:64].reshape(DEPTH, NSB, 3, D) for i in range(NCORES)], axis=1)
    conv_b_sample = np.concatenate([R[i]["sso"][:, 64:96].reshape(DEPTH, NSB, 2, D) for i in range(NCORES)], axis=1)
    return tuple(np.ascontiguousarray(a.astype(np.float32)) for a in
                 (y_prompt, y_sample, h_prompt, conv_a_prompt, conv_b_prompt, h_sample, conv_a_sample, conv_b_sample))
```

```python
import contextlib
import numpy as np
import concourse.bass as bass
import concourse.mybir as mybir
from concourse.bass_utils import run_bass_kernel_spmd

F32 = mybir.dt.float32
BF16 = mybir.dt.bfloat16
AF = mybir.ActivationFunctionType
ALU = mybir.AluOpType

D = 1024
NCH = 8
DEPTH = 4
NCORES = 8
SEQ = 2048
NSB = 16
ST = 4
ALPHA = (2.0 * DEPTH) ** 0.25
LN_EPS = 1e-6
UNITS_PER_LAYER = 45
TW = 1088

STAT_FP32R = True
CFG = dict(tslots=4, mulalt=True, groupwise=True, nwb=8, single_bx=False, ln_dedicated=False, lnr=4, window=100)


class Buf:
    __slots__ = ("name", "w", "rs", "const")

    def __init__(self, name):
        self.name = name
        self.w = None
        self.rs = []
        self.const = False


class Op:
    __slots__ = ("eng", "fn", "deps", "sem", "val", "inc", "cost", "lat", "tbl", "idx", "start", "finish", "dma")


ENGS = ("pe", "act", "dve", "pool", "sp")
SYNC_LAT = 0.25
TBL_SWITCH = 1.3
WINDOW = 48
SCHEDULE = True
PRIO_Q = 0.4


class Prog:
    def __init__(self, nc, stack):
        self.nc = nc
        self.stack = stack
        self.q = {e: [] for e in ENGS}
        self.esem = {e: stack.enter_context(nc.semaphore("sem_" + e)) for e in ENGS}
        self.dsem = {}
        self.dcnt = {}
        self.finals = []
        self.nops = 0

    def add(self, eng, fn, reads=(), writes=(), dma=None, cost=0.3, lat=0.0, tbl=0):
        deps = []
        for b in reads:
            if b.w is not None:
                deps.append(b.w)
        for b in writes:
            if b.w is not None:
                deps.append(b.w)
            deps.extend(b.rs)
        op = Op()
        op.eng = eng
        op.fn = fn
        op.cost = cost
        op.lat = lat
        op.tbl = tbl
        op.idx = self.nops
        self.nops += 1
        op.dma = dma
        op.start = None
        op.finish = None
        if dma is None:
            op.sem = self.esem[eng]
            op.val = None
            op.inc = 1
        else:
            if dma not in self.dsem:
                self.dsem[dma] = self.stack.enter_context(self.nc.semaphore("dsem_" + dma))
                self.dcnt[dma] = 0
            self.dcnt[dma] += 16
            op.sem = self.dsem[dma]
            op.val = self.dcnt[dma]
            op.inc = 16
        seen = set()
        dl = []
        for d in deps:
            if id(d) not in seen and d is not op:
                seen.add(id(d))
                dl.append(d)
        op.deps = dl
        for b in reads:
            if not b.const:
                b.rs.append(op)
        for b in writes:
            b.w = op
            b.rs = []
        self.q[eng].append(op)
        return op

    def schedule(self):
        pend = {e: list(self.q[e]) for e in ENGS}
        allops = sorted((o for e in ENGS for o in self.q[e]), key=lambda o: o.idx)
        bl = {}
        succ_max = {}
        for o in reversed(allops):
            b_ = o.cost + o.lat + succ_max.get(id(o), 0.0)
            bl[id(o)] = b_
            for d in o.deps:
                v = b_ + (0.0 if d.eng == o.eng else SYNC_LAT)
                if succ_max.get(id(d), 0.0) < v:
                    succ_max[id(d)] = v
        head = {e: 0 for e in ENGS}
        free = {e: 0.0 for e in ENGS}
        newq = {e: [] for e in ENGS}
        cur_tbl = 0
        remaining = sum(len(v) for v in pend.values())
        done = {e: [False] * len(pend[e]) for e in ENGS}
        while remaining:
            best = None
            for e in ENGS:
                lst = pend[e]
                n = len(lst)
                h = head[e]
                while h < n and done[e][h]:
                    h += 1
                head[e] = h
                cnt = 0
                i = h
                cand = None
                while i < n and cnt < CFG['window']:
                    if not done[e][i]:
                        cnt += 1
                        op = lst[i]
                        rdy = 0.0
                        ok = True
                        for d in op.deps:
                            if d.finish is None:
                                ok = False
                                break
                            t = d.finish if d.eng == e and d.dma is None else d.finish + SYNC_LAT
                            if t > rdy:
                                rdy = t
                        if ok:
                            st = rdy if rdy > free[e] else free[e]
                            sw = 1 if (e == "act" and op.tbl != 0 and op.tbl != cur_tbl) else 0
                            key = (round((st + (TBL_SWITCH if sw else 0.0)) / PRIO_Q), -bl[id(op)], op.idx)
                            if cand is None or key < cand[0]:
                                cand = (key, st, sw, i, op)
                    i += 1
                if cand is not None and (best is None or cand[0] < best[1][0]):
                    best = (e, cand)
            assert best is not None, "scheduler deadlock"
            e, (key, st, sw, i, op) = best
            if sw:
                st += TBL_SWITCH
                cur_tbl = op.tbl
            elif e == "act" and op.tbl != 0:
                cur_tbl = op.tbl
            op.start = st
            free[e] = st + op.cost
            op.finish = st + op.cost + op.lat
            done[e][i] = True
            newq[e].append(op)
            remaining -= 1
        self.q = newq
        return max(free.values())

    def emit(self):
        nc = self.nc
        prog = self
        for e in ENGS:
            k = 0
            for op in self.q[e]:
                if op.dma is None:
                    k += 1
                    op.val = k

        def run(ename, eng):
            waited = {}
            for op in prog.q[ename]:
                for d in op.deps:
                    if ename == "pe" and d.eng == "pe" and d.dma is None:
                        continue
                    k = id(d.sem)
                    if waited.get(k, 0) < d.val:
                        eng.wait_ge(d.sem, d.val)
                        waited[k] = d.val
                ins = op.fn(eng)
                ins.then_inc(op.sem, op.inc)
            if ename == "sp":
                for op in prog.finals:
                    k = id(op.sem)
                    if waited.get(k, 0) < op.val:
                        eng.wait_ge(op.sem, op.val)
                        waited[k] = op.val

        with nc.Block() as block:
            @block.tensor
            def _(e):
                run("pe", e)

            @block.scalar
            def _(e):
                run("act", e)

            @block.vector
            def _(e):
                run("dve", e)

            @block.gpsimd
            def _(e):
                run("pool", e)

            @block.sync
            def _(e):
                run("sp", e)


def bc_last(ap, n):
    return bass.AP(ap.tensor, ap.offset, [list(x) for x in ap.ap] + [[0, n]])


def build_program():
    nc = bass.Bass("TRN2", target_bir_lowering=False)
    with contextlib.ExitStack() as stack:
        _build(nc, stack)
    return nc


def _build(nc, stack):
    P = Prog(nc, stack)

    def dram(name, shape, kind):
        return nc.dram_tensor(name, shape, F32, kind=kind).ap()

    xp_d = dram("xp", [SEQ, D], "ExternalInput")
    xs_d = dram("xs", [NSB * ST, D], "ExternalInput")
    sst_d = dram("sst", [DEPTH, 96, D], "ExternalInput")
    cvec_d = dram("cvec", [17, D], "ExternalInput")
    prm_d = dram("prm", [64, D], "ExternalInput")
    wmain_d = dram("wmain", [DEPTH * UNITS_PER_LAYER, 128, 2048], "ExternalInput")
    wc_d = dram("wc", [DEPTH * 12, 128, 2048], "ExternalInput")
    ident_d = dram("ident", [128, 128], "ExternalInput")
    yp_d = dram("yp", [SEQ, D], "ExternalOutput")
    ys_d = dram("ys", [NSB * ST, D], "ExternalOutput")
    pst_d = dram("psto", [DEPTH, 6, D], "ExternalOutput")
    sso_d = dram("sso", [DEPTH, 96, D], "ExternalOutput")

    def sb(name, shape, dt=F32):
        return stack.enter_context(nc.sbuf_tensor(name, shape, dt))

    X = sb("X", [128, NCH, TW])
    HN = sb("HN", [128, NCH, TW], BF16)
    YA = sb("YA", [128, NCH, TW], BF16)
    YB = sb("YB", [128, NCH, TW], BF16)
    MGW = 576 if CFG["groupwise"] else TW
    MG = sb("MG", [128, NCH, MGW], BF16)
    XB = [[Buf(f"X{c}_{n}") for n in range(3)] for c in range(NCH)]
    HNB = [[Buf(f"HN{c}_{n}") for n in range(3)] for c in range(NCH)]
    YAB = [[Buf(f"YA{c}_{n}") for n in range(3)] for c in range(NCH)]
    YBB = [[Buf(f"YB{c}_{n}") for n in range(3)] for c in range(NCH)]
    MGB = [[Buf(f"MG{c}_{n}") for n in range(3)] for c in range(NCH)]

    NWB = CFG['nwb']
    WB = [sb(f"WB{i}", [128, 2048], BF16) for i in range(NWB)]
    WBB = [Buf(f"WB{i}") for i in range(NWB)]
    WG = sb("WG", [128, 2048], BF16)
    WGB = Buf("WG")

    PRM = sb("PRM", [128, NCH, 64])
    PRMB = Buf("PRM")
    SST = sb("SST", [128, DEPTH, NCH, 96])
    SSTB = [[Buf(f"SST{l}_{c}") for c in range(NCH)] for l in range(DEPTH)]
    PST = sb("PST", [128, DEPTH, NCH, 6])
    PSTB = [[Buf(f"PST{l}_{c}") for c in range(NCH)] for l in range(DEPTH)]
    MOD = sb("MOD", [128, DEPTH, 24, 17])
    MODB = [Buf(f"MOD{l}") for l in range(DEPTH)]
    SCT = sb("SCT", [128, NCH, 17])
    SCTB = Buf("SCT")
    SH = sb("SH", [128, DEPTH, NCH])
    S1 = sb("S1", [128, DEPTH, NCH])
    HBR = sb("HBR", [128, DEPTH, NCH])
    HBI = sb("HBI", [128, DEPTH, NCH])
    DERB = Buf("DER")
    IDN = sb("IDN", [128, 128])
    IDNB = Buf("IDN")
    ONES = sb("ONES", [128, 128])
    ONESB = Buf("ONES")

    nP1 = 18 if CFG["single_bx"] else 20
    NT = nP1 + (5 if CFG["ln_dedicated"] else 0)
    LN0 = nP1 if CFG["ln_dedicated"] else 0
    TMPALL = sb("TMPALL", [128, NT * 516])
    TMP = [TMPALL[:, i * 516:(i + 1) * 516] for i in range(NT)]
    TMPB = [Buf(f"TMP{i}") for i in range(NT)]
    IOB = [TMPALL[:, i * 516:i * 516 + 1024] for i in (14, 16, 18)]
    IOBB = [TMPB[14:16], TMPB[16:18], TMPB[18:20]]
    XAb = sb("XAb", [128, 512], BF16)
    LNR = sb("LNR", [128, CFG["lnr"], 512])
    LNRB = [Buf(f"LNR{i}") for i in range(CFG["lnr"])]
    F32R = mybir.dt.float32r
    XAbB = Buf("XAb")

    PS = [stack.enter_context(nc.psum_tensor(f"PS{i}", [128, 1024], F32)) for i in range(4)]
    BANKB = [Buf(f"BANK{i}") for i in range(8)]

    def bank(i):
        return PS[i // 2][:, (i % 2) * 512:(i % 2) * 512 + 512]

    state = {"pair": 0, "stg": 0, "io": 0}

    def next_pair():
        k = state["pair"]
        state["pair"] = (k + 1) % 4
        return k

    def next_stg():
        k = state["stg"]
        state["stg"] = (k + 1) % 2
        return k

    def next_io():
        k = state["io"]
        state["io"] = (k + 1) % 3
        return k

    def tiles_of(s):
        if s == 0:
            return [(0, 0, 512, "p"), (1, 512, 512, "p"), (2, 1024, 64, "s")]
        return [(0, 0, 512, "p"), (1, 512, 512, "p")]

    def groups_of(s):
        if not CFG["groupwise"]:
            return [tiles_of(s)]
        if s == 0:
            return [[(0, 0, 512, "p")], [(1, 512, 512, "p"), (2, 1024, 64, "s")]]
        return [[(0, 0, 512, "p")], [(1, 512, 512, "p")]]

    seq = []
    for u in range(12):
        seq.append((wc_d[u], False))
    for s in range(2):
        for l in range(DEPTH):
            base = l * UNITS_PER_LAYER
            pre = (s == 0 and l < DEPTH - 1)
            seq.append((wmain_d[base], True))
            for c in range(NCH):
                for i in range(3):
                    seq.append((wmain_d[base + 1 + 3 * c + i], False))
                if pre and 1 <= c < 7:
                    seq.append((wc_d[(l + 1) * 12 + 2 * (c - 1)], False))
                    seq.append((wc_d[(l + 1) * 12 + 2 * (c - 1) + 1], False))
            for gi in range(len(groups_of(s))):
                for j in range(NCH):
                    for i in range(2):
                        seq.append((wmain_d[base + 25 + 2 * j + i], False))
                for q in range(4):
                    seq.append((wmain_d[base + 41 + q], False))
    isgate = [g for (_, g) in seq]
    ngidx = []
    cnt_ = 0
    for g_ in isgate:
        ngidx.append(cnt_)
        if not g_:
            cnt_ += 1
    ws = {"next_load": 0, "next_get": 0, "released": 0, "slot": {}}

    def ws_pump():
        while True:
            j = ws["next_load"]
            if j >= len(seq):
                break
            if not (isgate[j] or ngidx[j] < ws["released"] + NWB):
                break
            if isgate[j]:
                dst, dB, slot, tag = WG, WGB, -1, "wg"
            else:
                slot = ngidx[j] % NWB
                dst, dB, tag = WB[slot], WBB[slot], f"wb{slot}"
            srcd = seq[j][0]
            P.add("pool", lambda e, dst=dst, srcd=srcd: e.dma_start(out=dst[:, :], in_=srcd),
                  writes=[dB], dma=tag, cost=0.65, lat=5.0)
            ws["slot"][j] = slot
            ws["next_load"] += 1

    def ws_get():
        ws_pump()
        i = ws["next_get"]
        assert i in ws["slot"], (i, ws["next_load"], ws["released"])
        slot = ws["slot"].pop(i)
        ws["next_get"] += 1
        if slot < 0:
            return WG, WGB
        return WB[slot], WBB[slot]

    def ws_release(n=1):
        ws["released"] += n
        ws_pump()

    def fs(ap):
        return ap.free_size()

    TBL = {AF.Tanh: 1, AF.Exp: 1, AF.Sqrt: 2, AF.Ln: 3}

    def act(out, in_, func, reads, writes, bias=None, scale=None):
        kw = {}
        nap = 0
        if bias is not None:
            kw["bias"] = bias
            nap += 0 if isinstance(bias, float) else 1
        if scale is not None:
            kw["scale"] = scale
            nap += 0 if isinstance(scale, float) else 1
        cost = 0.18 + 0.09 * nap + fs(out) * 0.00066
        return P.add("act", lambda e: e.activation(out=out, in_=in_, func=func, **kw), reads, writes,
                     cost=cost, tbl=TBL.get(func, 0))

    def ew_cost(eng, out, two_in, stt_=False):
        if eng == "pool":
            return 0.2 + fs(out) * (0.0021 if two_in else 0.0009)
        if stt_:
            return 0.2 + fs(out) * 0.00127
        if two_in:
            return 0.1 + fs(out) * 0.00095
        return 0.1 + fs(out) * 0.00065

    def tt(eng, out, in0, in1, op, reads, writes):
        return P.add(eng, lambda e: e.tensor_tensor(out=out, in0=in0, in1=in1, op=op), reads, writes,
                     cost=ew_cost(eng, out, True))

    def ts(eng, out, in0, s1, s2, op0, op1, reads, writes):
        if op1 is None:
            return P.add(eng, lambda e: e.tensor_scalar(out=out, in0=in0, scalar1=s1, scalar2=None, op0=op0),
                         reads, writes, cost=ew_cost(eng, out, False))
        return P.add(eng, lambda e: e.tensor_scalar(out=out, in0=in0, scalar1=s1, scalar2=s2, op0=op0, op1=op1),
                     reads, writes, cost=ew_cost(eng, out, False))

    def stt(out, in0, scalar, in1, op0, op1, reads, writes):
        return P.add("dve", lambda e: e.scalar_tensor_tensor(out=out, in0=in0, scalar=scalar, in1=in1,
                                                             op0=op0, op1=op1), reads, writes,
                     cost=ew_cost("dve", out, True, True))

    def cp(eng, out, in_, reads, writes):
        if eng == "act":
            return act(out, in_, AF.Copy, reads, writes)
        return P.add(eng, lambda e: e.tensor_copy(out=out, in_=in_), reads, writes, cost=ew_cost(eng, out, False))

    def mm_group(out, pairs, reads, writes, passes=1):
        def fn(e):
            ins = None
            n = len(pairs)
            for i, (l_, r_) in enumerate(pairs):
                ins = e.matmul(out, l_, r_, start=(i == 0), stop=(i == n - 1))
            return ins
        per = max(0.065, 0.012 + fs(out) * 0.00043) * passes
        return P.add("pe", fn, reads, writes, cost=per * len(pairs), lat=0.15)

    def transpose(out, in_, ident, reads, writes):
        return P.add("pe", lambda e: e.transpose(out, in_, ident), reads, writes, cost=0.12, lat=0.15)

    P.add("sp", lambda e: e.dma_start(out=IDN[:, :], in_=ident_d), writes=[IDNB], dma="misc", cost=0.1, lat=2.5)
    P.add("pool", lambda e: e.memset(ONES[:, :], 1.0), writes=[ONESB])
    if STAT_FP32R:
        P.add("act", lambda e: e.activation(out=ONES[:, :].bitcast(F32R), in_=ONES[:, :], func=AF.Copy), [ONESB], [ONESB])
    P.add("pool", lambda e: e.memset(PST[:, :, :, :], 0.0),
          writes=[PSTB[l][c] for l in range(DEPTH) for c in range(NCH)])

    def act_copy(out, in_, reads, writes, scale=None):
        return act(out, in_, AF.Copy, reads, writes, scale=scale)

    def load_rows_T(src_ap, nrows):
        k = next_io()
        io = IOB[k]
        P.add("sp", lambda e: e.dma_start(out=io[:nrows, :], in_=src_ap), writes=IOBB[k], dma=f"io{k}", cost=0.1, lat=3.0)
        pr = next_pair()
        pst = PS[pr]
        bb = [BANKB[2 * pr], BANKB[2 * pr + 1]]
        for c in range(NCH):
            transpose(pst[:, c * 128:c * 128 + nrows], io[:nrows, c * 128:(c + 1) * 128], IDN[:nrows, :nrows],
                      reads=IOBB[k] + [IDNB], writes=bb)
        view = pst[:, :].rearrange("p (c r) -> p c r", r=128)[:, :, 0:nrows]
        return view, bb

    def prologue_a():
        view_, bb_ = load_rows_T(prm_d, 64)
        cp("dve", PRM[:, :, :], view_, bb_, [PRMB])
        for l in range(DEPTH):
            view_, bb_ = load_rows_T(sst_d[l], 96)
            if l % 2 == 0:
                cp("dve", SST[:, l, :, :], view_, bb_, [SSTB[l][c] for c in range(NCH)])
            else:
                act_copy(SST[:, l, :, :], view_, bb_, [SSTB[l][c] for c in range(NCH)])

        def prm_col(l, idx):
            return PRM[:, :, l * 16 + idx]

        for l in range(DEPTH):
            act(S1[:, l, :], prm_col(l, 7), AF.Exp, [PRMB], [DERB], scale=-1.0)
            act(S1[:, l, :], S1[:, l, :], AF.Ln, [DERB], [DERB], bias=1.0)
            ts("dve", SH[:, l, :], S1[:, l, :], -4.0, None, ALU.mult, None, [DERB], [DERB])
            ts("dve", S1[:, l, :], S1[:, l, :], -8.0, None, ALU.mult, None, [DERB], [DERB])
            ts("dve", HBR[:, l, :], prm_col(l, 5), 0.5, None, ALU.mult, None, [PRMB], [DERB])
            ts("dve", HBI[:, l, :], prm_col(l, 6), 0.5, None, ALU.mult, None, [PRMB], [DERB])

        P.add("sp", lambda e: e.dma_start(out=IOB[0][:17, :], in_=cvec_d), writes=IOBB[0], dma="io0", cost=0.1, lat=3.0)
        act(IOB[1][:17, :], IOB[0][:17, :], AF.Tanh, IOBB[0], IOBB[1], scale=0.5)
        stt(IOB[1][:17, :], IOB[1][:17, :], 1.0, IOB[0][:17, :], ALU.add, ALU.mult, IOBB[0] + IOBB[1], IOBB[1])
        pr_ = next_pair()
        for c in range(NCH):
            transpose(PS[pr_][:, c * 17:(c + 1) * 17], IOB[1][:17, c * 128:(c + 1) * 128], IDN[:17, :17],
                      reads=IOBB[1] + [IDNB], writes=[BANKB[2 * pr_], BANKB[2 * pr_ + 1]])
        cp("dve", SCT[:, :, :], PS[pr_][:, 0:136].rearrange("p (c r) -> p c r", r=17),
           [BANKB[2 * pr_], BANKB[2 * pr_ + 1]], [SCTB])

    SCTb = sb("SCTb", [128, NCH, 17], BF16)
    MTs = sb("MTs", [128, 256])
    MTsB = Buf("MTs")

    def mod_unit(l, u):
        wb, wbB = ws_get()
        pa = next_pair()
        acc = PS[pa][:17, 0:256]
        mm_group(acc, [(SCTb[:, kc, :], wb[:, kc * 256:(kc + 1) * 256]) for kc in range(NCH)],
                 reads=[SCTB, wbB], writes=[BANKB[2 * pa]])
        ws_release(1)
        act_copy(MTs[:17, :], acc, [BANKB[2 * pa]], [MTsB], scale=0.5)
        pt = next_pair()
        ptb = [BANKB[2 * pt]]
        for h in range(2):
            transpose(PS[pt][:, h * 17:(h + 1) * 17], MTs[:17, h * 128:(h + 1) * 128], IDN[:17, :17],
                      reads=[MTsB, IDNB], writes=ptb)
        j0 = 2 * u
        part, cc0 = j0 // 8, j0 % 8
        dst = MOD[:, l, j0:j0 + 2, :]
        tt("dve", dst, PS[pt][:, 0:34].rearrange("p (c r) -> p c r", r=17),
           bc_last(PRM[:, cc0:cc0 + 2, l * 16 + 13 + part], 17), ALU.add, ptb + [PRMB], [MODB[l]])
        if part == 1:
            ts("dve", dst, dst, 1.0, None, ALU.add, None, [MODB[l]], [MODB[l]])
        elif part == 2:
            ts("dve", dst, dst, 0.25 / ALPHA, None, ALU.mult, None, [MODB[l]], [MODB[l]])

    def prologue_b():
        cp("dve", SCTb[:, :, :], SCT[:, :, :], [SCTB], [SCTB])
        for u in range(12):
            mod_unit(0, u)
        for b in [PRMB, DERB, IDNB, ONESB, SCTB]:
            b.const = True

    def v3(ap, inner):
        return ap.rearrange("p (b t) -> p b t", t=inner)

    def ln_stats(n, c0, W, eps, tm, tr, tsq):
        pr = next_pair()
        b1, b2 = 2 * pr, 2 * pr + 1
        ps1 = bank(b1)[:, 0:W]
        ps2 = bank(b2)[:, 0:W]
        if STAT_FP32R:
            ones = ONES[:, :].bitcast(F32R)
            for c in range(NCH):
                if CFG['lnr'] == 4:
                    qx, qs = c % 2, 2 + (c % 2)
                else:
                    qx, qs = c % 4, 4 + (c % 4)
                xr = LNR[:, qx, 0:W].bitcast(F32R)
                sq = LNR[:, qs, 0:W].bitcast(F32R)
                cp("dve", xr, X[:, c, c0:c0 + W], [XB[c][n]], [LNRB[qx]])
                P.add("pe", lambda e, c=c, xr=xr: e.matmul(ps1, ones, xr, start=(c == 0), stop=(c == NCH - 1)),
                      reads=[LNRB[qx], ONESB], writes=[BANKB[b1]], cost=0.24, lat=0.15)
                act(sq, X[:, c, c0:c0 + W], AF.Square, [XB[c][n]], [LNRB[qs]])
                P.add("pe", lambda e, c=c, sq=sq: e.matmul(ps2, ones, sq, start=(c == 0), stop=(c == NCH - 1)),
                      reads=[LNRB[qs], ONESB], writes=[BANKB[b2]], cost=0.24, lat=0.15)
        else:
            ones = ONES[:, :]
            mm_group(ps1, [(ones, X[:, c, c0:c0 + W]) for c in range(NCH)],
                     reads=[XB[c][n] for c in range(NCH)] + [ONESB], writes=[BANKB[b1]], passes=4)
            for c in range(NCH):
                q = tsq[c % 2]
                act(TMP[q][:, 0:W], X[:, c, c0:c0 + W], AF.Square, [XB[c][n]], [TMPB[q]])
                P.add("pe", lambda e, c=c, q=q: e.matmul(ps2, ones, TMP[q][:, 0:W], start=(c == 0),
                                                         stop=(c == NCH - 1)),
                      reads=[TMPB[q], ONESB], writes=[BANKB[b2]], cost=0.95, lat=0.15)
        M = TMP[tm][:, 0:W]
        R = TMP[tr][:, 0:W]
        MM = TMP[tsq[0]][:, 0:W]
        act_copy(M, ps1, [BANKB[b1]], [TMPB[tm]], scale=1.0 / D)
        act(MM, ps1, AF.Square, [BANKB[b1]], [TMPB[tsq[0]]], scale=1.0 / D)
        stt(R, ps2, 1.0 / D, MM, ALU.mult, ALU.subtract, [BANKB[b2], TMPB[tsq[0]]], [TMPB[tr]])
        act(R, R, AF.Ln, [TMPB[tr]], [TMPB[tr]], bias=float(eps), scale=1.0)
        act(R, R, AF.Exp, [TMPB[tr]], [TMPB[tr]], scale=-0.5)
        return M, R

    def ln_in(l, n, c0, W, kind):
        o_ = LN0
        M, R = ln_stats(n, c0, W, LN_EPS, o_, o_ + 1, (o_ + 2, o_ + 2))
        for c in range(NCH):
            q = o_ + 3 + (c % CFG['tslots'])
            t = TMP[q][:, 0:W]
            tt("dve", t, X[:, c, c0:c0 + W], M, ALU.subtract, [XB[c][n], TMPB[o_]], [TMPB[q]])
            if kind == "p":
                tt("dve" if (CFG["mulalt"] and c % 2 == 1) else "pool", t, t, R, ALU.mult, [TMPB[q], TMPB[o_ + 1]], [TMPB[q]])
                act(HN[:, c, c0:c0 + W], t, AF.Identity, [TMPB[q], MODB[l]], [HNB[c][n]],
                    bias=MOD[:, l, c, 0:1], scale=MOD[:, l, 8 + c, 0:1])
            else:
                tt("dve", t, t, R, ALU.mult, [TMPB[q], TMPB[o_ + 1]], [TMPB[q]])
                tt("dve", v3(t, ST), v3(t, ST), bc_last(MOD[:, l, 8 + c, 1:17], ST), ALU.mult,
                   [TMPB[q], MODB[l]], [TMPB[q]])
                tt("dve", v3(HN[:, c, c0:c0 + W], ST), v3(t, ST), bc_last(MOD[:, l, c, 1:17], ST), ALU.add,
                   [TMPB[q], MODB[l]], [HNB[c][n]])

    def ln_out(l, n, c0, W):
        o_ = LN0
        M, R = ln_stats(n, c0, W, LN_EPS / (ALPHA * ALPHA), o_, o_ + 1, (o_ + 2, o_ + 2))
        for c in range(NCH):
            q = o_ + 3 + (c % CFG['tslots'])
            t = TMP[q][:, 0:W]
            tt("dve", t, X[:, c, c0:c0 + W], M, ALU.subtract, [XB[c][n], TMPB[o_]], [TMPB[q]])
            tt("pool" if (W == 512 and not (CFG["mulalt"] and c % 2 == 1)) else "dve", t, t, R, ALU.mult, [TMPB[q], TMPB[o_ + 1]], [TMPB[q]])
            act(X[:, c, c0:c0 + W], t, AF.Identity, [TMPB[q], PRMB], [XB[c][n]],
                bias=PRM[:, c, l * 16 + 12:l * 16 + 13], scale=PRM[:, c, l * 16 + 11:l * 16 + 12])

    def load_x(s):
        for (n, c0, W, kind) in tiles_of(s):
            if kind == "p":
                for tb in range(4):
                    k = next_io()
                    io = IOB[k]
                    r0 = s * 1024 + c0 + tb * 128
                    P.add("sp", lambda e, io=io, r0=r0: e.dma_start(out=io[:, :], in_=xp_d[r0:r0 + 128, :]),
                          writes=IOBB[k], dma=f"io{k}", cost=0.1, lat=3.5)
                    for c in range(NCH):
                        transpose(bank(c)[:, tb * 128:(tb + 1) * 128], io[:, c * 128:(c + 1) * 128], IDN[:, :],
                                  reads=IOBB[k] + [IDNB], writes=[BANKB[c]])
                for c in range(NCH):
                    if c % 2 == 0:
                        cp("dve", X[:, c, c0:c0 + W], bank(c)[:, 0:W], [BANKB[c]], [XB[c][n]])
                    else:
                        act_copy(X[:, c, c0:c0 + W], bank(c)[:, 0:W], [BANKB[c]], [XB[c][n]])
            else:
                k = next_io()
                io = IOB[k]
                P.add("sp", lambda e, io=io: e.dma_start(out=io[:64, :], in_=xs_d), writes=IOBB[k],
                      dma=f"io{k}", cost=0.1, lat=3.0)
                for c in range(NCH):
                    transpose(bank(0)[:, c * 64:(c + 1) * 64], io[:64, c * 128:(c + 1) * 128], IDN[:64, :64],
                              reads=IOBB[k] + [IDNB], writes=[BANKB[0]])
                cp("dve", X[:, :, c0:c0 + W], bank(0)[:, 0:512].rearrange("p (c r) -> p c r", r=64),
                   [BANKB[0]], [XB[c][n] for c in range(NCH)])

    def store_rows(nr, src_fn, srcbufs, dst_ap, use_act=False):
        pr = next_pair()
        pst = PS[pr]
        bb = [BANKB[2 * pr], BANKB[2 * pr + 1]]
        for c in range(NCH):
            transpose(pst[:nr, c * 128:(c + 1) * 128], src_fn(c), IDN[:, :],
                      reads=[srcbufs[c], IDNB], writes=bb)
        k = next_io()
        io = IOB[k]
        if use_act:
            act_copy(io[:nr, :], pst[:nr, :], bb, IOBB[k])
        else:
            cp("dve", io[:nr, :], pst[:nr, :], bb, IOBB[k])
        op = P.add("sp", lambda e: e.dma_start(out=dst_ap, in_=io[:nr, :]), reads=IOBB[k], dma=f"so{k}", cost=0.1, lat=3.5)
        P.finals.append(op)

    def store_x(s, n, c0, W, kind):
        if kind == "p":
            for tb in range(4):
                r0 = s * 1024 + c0 + tb * 128
                a0 = c0 + tb * 128
                store_rows(128, lambda c, a0=a0: X[:, c, a0:a0 + 128], [XB[c][n] for c in range(NCH)],
                           yp_d[r0:r0 + 128, :], use_act=(tb % 2 == 1))
        else:
            store_rows(64, lambda c: X[:, c, c0:c0 + 64], [XB[c][n] for c in range(NCH)], ys_d)

    def store_states():
        for l in range(DEPTH):
            store_rows(6, lambda c, l=l: PST[:, l, c, :], PSTB[l], pst_d[l])
            store_rows(96, lambda c, l=l: SST[:, l, c, :], SSTB[l], sso_d[l], use_act=True)

    T_AX = (0, 1)
    T_TZ = (2, 3)
    if CFG['single_bx']:
        T_BX = (4, 4)
        T_TBZ = (5, 5)
        T_XA = (6, 7)
        T_TR = (8, 9)
        T_TI = (10, 11)
        T_A = (12, 13)
        T_A2 = (14, 15)
        T_CB, T_YB1 = 16, 17
    else:
        T_BX = (4, 5)
        T_TBZ = (6, 7)
        T_XA = (8, 9)
        T_TR = (10, 11)
        T_TI = (12, 13)
        T_A = (14, 15)
        T_A2 = (16, 17)
        T_CB, T_YB1 = 18, 19

    def wcol(wb, kc, o):
        return wb[:, kc * 256 + o * 128:kc * 256 + (o + 1) * 128]

    def proj2(wb, wbB, src, srcB, n, c0, W):
        pr = next_pair()
        ids = (2 * pr, 2 * pr + 1)
        for o in range(2):
            mm_group(bank(ids[o])[:, 0:W], [(wcol(wb, kc, o), src[:, kc, c0:c0 + W]) for kc in range(NCH)],
                     reads=[wbB] + [srcB[kc][n] for kc in range(NCH)], writes=[BANKB[ids[o]]])
        return ids

    p1 = {"k": 0}

    def p1_stage1(l, c, n, c0, W, kind, wg, wgB, wbA, wbAB, wb1, wb1B, wb2, wb2B):
        def sc(idx):
            return PRM[:, c, l * 16 + idx:l * 16 + idx + 1]
        par = p1["k"] % 2
        p1["k"] += 1
        qAX, qTZ, qBX, qTBZ = T_AX[par], T_TZ[par], T_BX[par], T_TBZ[par]
        qXA, qTR, qTI, qA, qA2 = T_XA[par], T_TR[par], T_TI[par], T_A[par], T_A2[par]
        AX = TMP[qAX]
        TZ = TMP[qTZ][:, 0:W]
        BX = TMP[qBX][:, 0:W]
        TBZ = TMP[qTBZ][:, 0:W]
        XA = TMP[qXA][:, 0:W]
        TR = TMP[qTR][:, 0:W]
        TI = TMP[qTI][:, 0:W]
        A = TMP[qA][:, 0:W]
        A2 = TMP[qA2][:, 0:W]
        CB = TMP[T_CB]
        YB1 = TMP[T_YB1][:, 0:W]
        ids = proj2(wbA, wbAB, HN, HNB, n, c0, W)
        if kind == "p":
            cp("pool", AX[:, 0:3], PST[:, l, c, 1:4], [PSTB[l][c]], [TMPB[qAX]])
            act_copy(AX[:, 3:3 + W], bank(ids[0])[:, 0:W], [BANKB[ids[0]]], [TMPB[qAX]])
            cp("pool", PST[:, l, c, 1:4], AX[:, W:W + 3], [TMPB[qAX]], [PSTB[l][c]])
            sl = lambda k: AX[:, k:k + W]
            XAo = XA
        else:
            AXv = v3(AX[:, 0:112], 7)
            cp("pool", AXv[:, :, 0:3], v3(SST[:, l, c, 16:64], 3), [SSTB[l][c]], [TMPB[qAX]])
            act_copy(AXv[:, :, 3:7], v3(bank(ids[0])[:, 0:W], ST), [BANKB[ids[0]]], [TMPB[qAX]])
            cp("pool", v3(SST[:, l, c, 16:64], 3), AXv[:, :, 4:7], [TMPB[qAX]], [SSTB[l][c]])
            sl = lambda k: AXv[:, :, k:k + ST]
            XAo = v3(XA, ST)
        act(XAo, sl(3), AF.Identity, [TMPB[qAX], PRMB], [TMPB[qXA]], bias=sc(4), scale=sc(3))
        for k in (2, 1, 0):
            stt(XAo, sl(k), sc(k), XAo, ALU.mult, ALU.add, [TMPB[qAX], PRMB, TMPB[qXA]], [TMPB[qXA]])
        act_copy(XAb[:, 0:W], XA, [TMPB[qXA]], [XAbB])
        az = bank(ids[1])[:, 0:W]
        act(TZ, az, AF.Tanh, [BANKB[ids[1]]], [TMPB[qTZ]], scale=0.5)
        stt(TZ, TZ, 1.0, az, ALU.add, ALU.mult, [TMPB[qTZ], BANKB[ids[1]]], [TMPB[qTZ]])
        ids1 = proj2(wb1, wb1B, HN, HNB, n, c0, W)
        act_copy(BX, bank(ids1[1])[:, 0:W], [BANKB[ids1[1]]], [TMPB[qBX]])
        cg = bank(ids1[0])[:, 0:W]
        if kind == "p":
            cp("pool", CB[:, 0:2], PST[:, l, c, 4:6], [PSTB[l][c]], [TMPB[T_CB]])
            tt("dve", CB[:, 2:2 + W], cg, BX, ALU.mult, [BANKB[ids1[0]], TMPB[qBX]], [TMPB[T_CB]])
            cp("pool", PST[:, l, c, 4:6], CB[:, W:W + 2], [TMPB[T_CB]], [PSTB[l][c]])
            slb = lambda k: CB[:, k:k + W]
            YBo = YB1
        else:
            CBv = v3(CB[:, 0:96], 6)
            cp("pool", CBv[:, :, 0:2], v3(SST[:, l, c, 64:96], 2), [SSTB[l][c]], [TMPB[T_CB]])
            tt("dve", CBv[:, :, 2:6], v3(cg, ST), v3(BX, ST), ALU.mult, [BANKB[ids1[0]], TMPB[qBX]],
               [TMPB[T_CB]])
            cp("pool", v3(SST[:, l, c, 64:96], 2), CBv[:, :, 4:6], [TMPB[T_CB]], [SSTB[l][c]])
            slb = lambda k: CBv[:, :, k:k + ST]
            YBo = v3(YB1, ST)
        ids2 = proj2(wb2, wb2B, HN, HNB, n, c0, W)
        bz = bank(ids2[1])[:, 0:W]
        act(TBZ, bz, AF.Tanh, [BANKB[ids2[1]]], [TMPB[qTBZ]], scale=0.5)
        stt(TBZ, TBZ, 1.0, bz, ALU.add, ALU.mult, [TMPB[qTBZ], BANKB[ids2[1]]], [TMPB[qTBZ]])
        tt("dve", TBZ, bank(ids2[0])[:, 0:W], TBZ, ALU.mult, [BANKB[ids2[0]], TMPB[qTBZ]], [TMPB[qTBZ]])
        act(YBo, slb(2), AF.Identity, [TMPB[T_CB], PRMB], [TMPB[T_YB1]], scale=sc(10))
        for k in (1, 0):
            stt(YBo, slb(k), sc(8 + k), YBo, ALU.mult, ALU.add, [TMPB[T_CB], PRMB, TMPB[T_YB1]],
                [TMPB[T_YB1]])
        tt("pool", YB[:, c, c0:c0 + W], YB1, TBZ, ALU.mult, [TMPB[T_YB1], TMPB[qTBZ]], [YBB[c][n]])
        pr = next_pair()
        gid = (2 * pr, 2 * pr + 1)
        mm_group(bank(gid[0])[:, 0:W], [(wg[:, c * 256:c * 256 + 128], XAb[:, 0:W])],
                 reads=[wgB, XAbB], writes=[BANKB[gid[0]]])
        mm_group(bank(gid[1])[:, 0:W], [(wg[:, c * 256 + 128:c * 256 + 256], XAb[:, 0:W])],
                 reads=[wgB, XAbB], writes=[BANKB[gid[1]]])
        act(TR, bank(gid[0])[:, 0:W], AF.Tanh, [BANKB[gid[0]], DERB], [TMPB[qTR]],
            bias=HBR[:, l, c:c + 1], scale=0.5)
        act(TI, bank(gid[1])[:, 0:W], AF.Tanh, [BANKB[gid[1]], DERB], [TMPB[qTI]],
            bias=HBI[:, l, c:c + 1], scale=0.5)
        act(A, TR, AF.Exp, [TMPB[qTR], DERB], [TMPB[qA]], bias=SH[:, l, c:c + 1], scale=SH[:, l, c:c + 1])
        act(A2, TR, AF.Exp, [TMPB[qTR], DERB], [TMPB[qA2]], bias=S1[:, l, c:c + 1], scale=S1[:, l, c:c + 1])
        stt(TI, TI, 1.0, XA, ALU.add, ALU.mult, [TMPB[qTI], TMPB[qXA]], [TMPB[qTI]])
        return dict(l=l, c=c, n=n, c0=c0, W=W, kind=kind, par=par)

    def p1_stage2(cx):
        l, c, n, c0, W, kind, par = cx["l"], cx["c"], cx["n"], cx["c0"], cx["W"], cx["kind"], cx["par"]
        qTZ, qXA, qTR, qTI, qA, qA2 = T_TZ[par], T_XA[par], T_TR[par], T_TI[par], T_A[par], T_A2[par]
        TZ = TMP[qTZ][:, 0:W]
        TR = TMP[qTR][:, 0:W]
        TI = TMP[qTI][:, 0:W]
        A = TMP[qA][:, 0:W]
        A2 = TMP[qA2][:, 0:W]
        act(A2, A2, AF.Sqrt, [TMPB[qA2]], [TMPB[qA2]], bias=0.25, scale=-0.25)
        tt("pool", TI, TI, A2, ALU.mult, [TMPB[qTI], TMPB[qA2]], [TMPB[qTI]])
        HS = TR
        if kind == "p":
            P.add("dve", lambda e: e.tensor_tensor_scan(
                out=HS, data0=A, data1=TI, initial=PST[:, l, c, 0:1], op0=ALU.mult, op1=ALU.add),
                [TMPB[qA], TMPB[qTI], PSTB[l][c]], [TMPB[qTR]], cost=0.1 + W * 0.0021)
            cp("pool", PST[:, l, c, 0:1], HS[:, W - 1:W], [TMPB[qTR]], [PSTB[l][c]])
        else:
            Av = v3(A, ST)
            Tv = v3(TI, ST)
            h0 = SST[:, l, c, 0:16].rearrange("p (b o) -> p b o", o=1)
            tmpv = v3(TMP[qXA][:, 0:16], 1)
            tt("dve", tmpv, Av[:, :, 0:1], h0, ALU.mult, [TMPB[qA], SSTB[l][c], TMPB[qXA]], [TMPB[qXA]])
            tt("dve", Tv[:, :, 0:1], Tv[:, :, 0:1], tmpv, ALU.add, [TMPB[qTI], TMPB[qXA]], [TMPB[qTI]])
            P.add("dve", lambda e: e.memset(Av[:, :, 0:1], 0.0), [], [TMPB[qA]])
            P.add("dve", lambda e: e.tensor_tensor_scan(
                out=HS, data0=A, data1=TI, initial=0.0, op0=ALU.mult, op1=ALU.add),
                [TMPB[qA], TMPB[qTI]], [TMPB[qTR]], cost=0.1 + W * 0.0021)
            cp("pool", h0, v3(HS, ST)[:, :, ST - 1:ST], [TMPB[qTR]], [SSTB[l][c]])
        tt("pool", YA[:, c, c0:c0 + W], HS, TZ, ALU.mult, [TMPB[qTR], TMPB[qTZ]], [YAB[c][n]])

    def phase1(l, s):
        tiles = tiles_of(s)
        wg, wgB = ws_get()
        pending = []
        for c in range(NCH):
            wbA, wbAB = ws_get()
            wb1, wb1B = ws_get()
            wb2, wb2B = ws_get()
            for (n, c0, W, kind) in tiles:
                pending.append(p1_stage1(l, c, n, c0, W, kind, wg, wgB, wbA, wbAB, wb1, wb1B, wb2, wb2B))
                if len(pending) == 2:
                    for cx in pending:
                        p1_stage2(cx)
                    pending = []
            ws_release(3)
            if s == 0 and l < DEPTH - 1 and 1 <= c < 7:
                mod_unit(l + 1, 2 * (c - 1))
                mod_unit(l + 1, 2 * (c - 1) + 1)
        for cx in pending:
            p1_stage2(cx)

    def phase2(l, s, tiles, gi):
        P.tag = 'p2'
        T_TAs, T_TBs = (8, 9), (8, 9)
        T_M1 = (10, 11, 12)
        kk = 0
        for j in range(NCH):
            wa, waB = ws_get()
            for (n, c0, W, kind) in tiles:
                pr = next_pair()
                ids = (2 * pr, 2 * pr + 1)
                mm_group(bank(ids[0])[:, 0:W], [(wcol(wa, kc, 0), YA[:, kc, c0:c0 + W]) for kc in range(NCH)],
                         reads=[waB] + [YAB[kc][n] for kc in range(NCH)], writes=[BANKB[ids[0]]])
                mm_group(bank(ids[1])[:, 0:W], [(wcol(wa, kc, 1), HN[:, kc, c0:c0 + W]) for kc in range(NCH)],
                         reads=[waB] + [HNB[kc][n] for kc in range(NCH)], writes=[BANKB[ids[1]]])
                T_TA = T_TAs[kk % 2]
                kk += 1
                TA = TMP[T_TA][:, 0:W]
                M1 = TMP[T_M1[n]][:, 0:W]
                act(TA, bank(ids[1])[:, 0:W], AF.Tanh, [BANKB[ids[1]]], [TMPB[T_TA]], scale=0.5)
                stt(M1, TA, 1.0, bank(ids[0])[:, 0:W], ALU.add, ALU.mult, [TMPB[T_TA], BANKB[ids[0]]],
                    [TMPB[T_M1[n]]])
            ws_release(1)
            wb_, wbB_ = ws_get()
            for (n, c0, W, kind) in tiles:
                pr = next_pair()
                ids = (2 * pr, 2 * pr + 1)
                mm_group(bank(ids[0])[:, 0:W], [(wcol(wb_, kc, 0), YB[:, kc, c0:c0 + W]) for kc in range(NCH)],
                         reads=[wbB_] + [YBB[kc][n] for kc in range(NCH)], writes=[BANKB[ids[0]]])
                mm_group(bank(ids[1])[:, 0:W], [(wcol(wb_, kc, 1), HN[:, kc, c0:c0 + W]) for kc in range(NCH)],
                         reads=[wbB_] + [HNB[kc][n] for kc in range(NCH)], writes=[BANKB[ids[1]]])
                T_TB = T_TBs[kk % 2]
                kk += 1
                TB = TMP[T_TB][:, 0:W]
                M1 = TMP[T_M1[n]][:, 0:W]
                act(TB, bank(ids[1])[:, 0:W], AF.Tanh, [BANKB[ids[1]]], [TMPB[T_TB]], scale=0.5)
                stt(TB, TB, 1.0, bank(ids[0])[:, 0:W], ALU.add, ALU.mult, [TMPB[T_TB], BANKB[ids[0]]], [TMPB[T_TB]])
                m0 = mgoff(c0, kind)
                tt("dve", MG[:, j, m0:m0 + W], M1, TB, ALU.add, [TMPB[T_M1[n]], TMPB[T_TB]], [MGB[j][mgslot(n, kind)]])
            ws_release(1)

    def phase3(l, s, tiles, gi):
        for jj in range(4):
            wo, woB = ws_get()
            for (n, c0, W, kind) in tiles:
                m0 = mgoff(c0, kind)
                pr = next_pair()
                ids = (2 * pr, 2 * pr + 1)
                for o in range(2):
                    j = 2 * jj + o
                    po = bank(ids[o])[:, 0:W]
                    mm_group(po, [(wcol(wo, kc, o), MG[:, kc, m0:m0 + W]) for kc in range(NCH)],
                             reads=[woB] + [MGB[kc][mgslot(n, kind)] for kc in range(NCH)], writes=[BANKB[ids[o]]])
                    xs_ = X[:, j, c0:c0 + W]
                    if kind == "p":
                        stt(xs_, po, MOD[:, l, 16 + j, 0:1], xs_, ALU.mult, ALU.add,
                            [BANKB[ids[o]], MODB[l], XB[j][n]], [XB[j][n]])
                    else:
                        q = 13
                        t = TMP[q][:, 0:W]
                        tt("dve", v3(t, ST), v3(po, ST), bc_last(MOD[:, l, 16 + j, 1:17], ST), ALU.mult,
                           [BANKB[ids[o]], MODB[l]], [TMPB[q]])
                        tt("dve", xs_, xs_, t, ALU.add, [TMPB[q], XB[j][n]], [XB[j][n]])
            ws_release(1)
        for (n, c0, W, kind) in tiles:
            ln_out(l, n, c0, W)
            if l < DEPTH - 1:
                ln_in(l + 1, n, c0, W, kind)
            else:
                store_x(s, n, c0, W, kind)

    def mgoff(c0, kind):
        if not CFG["groupwise"]:
            return c0
        return 0 if kind == "p" else 512

    def mgslot(n, kind):
        if not CFG["groupwise"]:
            return n
        return 0 if kind == "p" else 2

    for s in range(2):
        load_x(s)
        if s == 0:
            prologue_a()
            prologue_b()
        for (n, c0, W, kind) in tiles_of(s):
            ln_in(0, n, c0, W, kind)
        for l in range(DEPTH):
            phase1(l, s)
            for gi, g in enumerate(groups_of(s)):
                phase2(l, s, g, gi)
                phase3(l, s, g, gi)
    store_states()
    if SCHEDULE:
        est = P.schedule()
        print(f"[kernel] scheduled: est {est:.0f} us, ops={P.nops}")
    P.emit()


def _unit(W, cols):
    sub = W[:, cols]
    return np.ascontiguousarray(sub.reshape(8, 128, 256).transpose(1, 0, 2).reshape(128, 2048))


def _pack_weights(w_c, w_in, w_r, w_i, w_a_out, w_b_out, w_o):
    wmain = np.empty((DEPTH * UNITS_PER_LAYER, 128, 2048), np.float32)
    wc = np.empty((DEPTH * 12, 128, 2048), np.float32)
    ar = np.arange(128)
    for l in range(DEPTH):
        base = l * UNITS_PER_LAYER
        g = np.empty((128, 8, 256), np.float32)
        for c in range(8):
            g[:, c, 0:128] = w_r[l, c]
            g[:, c, 128:256] = w_i[l, c]
        wmain[base] = g.reshape(128, 2048)
        k = base + 1
        for c in range(8):
            cc = c * 128 + ar
            wmain[k] = _unit(w_in[l], np.concatenate([0 * 1024 + cc, 1 * 1024 + cc])); k += 1
            wmain[k] = _unit(w_in[l], np.concatenate([2 * 1024 + cc, 4 * 1024 + cc])); k += 1
            wmain[k] = _unit(w_in[l], np.concatenate([3 * 1024 + cc, 5 * 1024 + cc])); k += 1
        for j in range(8):
            jj = j * 128 + ar
            wmain[k] = np.concatenate(
                [w_a_out[l][:, jj].reshape(8, 128, 128), w_in[l][:, 6 * 1024 + jj].reshape(8, 128, 128)],
                axis=2).transpose(1, 0, 2).reshape(128, 2048); k += 1
            wmain[k] = np.concatenate(
                [w_b_out[l][:, jj].reshape(8, 128, 128), w_in[l][:, 7 * 1024 + jj].reshape(8, 128, 128)],
                axis=2).transpose(1, 0, 2).reshape(128, 2048); k += 1
        for q in range(4):
            wmain[k] = _unit(w_o[l], q * 256 + np.arange(256)); k += 1
        assert k == base + UNITS_PER_LAYER
        for u in range(12):
            wc[l * 12 + u] = _unit(w_c[l], u * 256 + np.arange(256))
    return wmain, wc


_NC_CACHE = {}


def kernel(x_prompt, x_sample, state_rglru_h, state_rglru_conv, state_sconv, c_prompt, c_sample,
           w_c, b_c, w_in, conv_a_w, conv_a_b, w_r, b_r, w_i, b_i, lru_lambda, conv_b_w,
           w_a_out, w_b_out, w_o, ln_g, ln_b):
    f = lambda a: np.ascontiguousarray(np.asarray(a, dtype=np.float32))
    x_prompt, x_sample = f(x_prompt), f(x_sample)
    state_rglru_h, state_rglru_conv, state_sconv = f(state_rglru_h), f(state_rglru_conv), f(state_sconv)
    c_prompt, c_sample = f(c_prompt), f(c_sample)
    wmain, wc = _pack_weights(f(w_c), f(w_in), f(w_r), f(w_i), f(w_a_out), f(w_b_out), f(w_o))
    prm = np.empty((DEPTH, 16, D), np.float32)
    prm[:, 0:4] = f(conv_a_w)
    prm[:, 4] = f(conv_a_b)
    prm[:, 5] = f(b_r)
    prm[:, 6] = f(b_i)
    prm[:, 7] = f(lru_lambda)
    prm[:, 8:11] = f(conv_b_w)
    prm[:, 11] = f(ln_g)
    prm[:, 12] = f(ln_b)
    prm[:, 13:16] = f(b_c).reshape(DEPTH, 3, D)
    prm = prm.reshape(64, D)
    ident = np.eye(128, dtype=np.float32)

    if "nc" not in _NC_CACHE:
        _NC_CACHE["nc"] = build_program()
    nc = _NC_CACHE["nc"]

    in_maps = []
    for i in range(NCORES):
        bs = slice(i * NSB, (i + 1) * NSB)
        sst = np.concatenate([
            state_rglru_h[:, bs],
            state_rglru_conv[:, bs].reshape(DEPTH, NSB * 3, D),
            state_sconv[:, bs].reshape(DEPTH, NSB * 2, D),
        ], axis=1)
        in_maps.append({
            "xp": x_prompt[i],
            "xs": x_sample[bs].reshape(NSB * ST, D),
            "sst": np.ascontiguousarray(sst),
            "cvec": np.ascontiguousarray(np.concatenate([c_prompt[i:i + 1], c_sample[bs]], axis=0)),
            "prm": prm,
            "wmain": wmain,
            "wc": wc,
            "ident": ident,
        })
    res = run_bass_kernel_spmd(nc, in_maps, core_ids=list(range(NCORES)))
    R = res.results
    y_prompt = np.stack([R[i]["yp"] for i in range(NCORES)], axis=0)
    y_sample = np.concatenate([R[i]["ys"].reshape(NSB, ST, D) for i in range(NCORES)], axis=0)
    h_prompt = np.stack([R[i]["psto"][:, 0] for i in range(NCORES)], axis=1)
    conv_a_prompt = np.stack([R[i]["psto"][:, 1:4] for i in range(NCORES)], axis=1)
    conv_b_prompt = np.stack([R[i]["psto"][:, 4:6] for i in range(NCORES)], axis=1)
    h_sample = np.concatenate([R[i]["sso"][:, 0:16] for i in range(NCORES)], axis=1)
    conv_a_sample = np.concatenate([R[i]["sso"][:, 16:64].reshape(DEPTH, NSB, 3, D) for i in range(NCORES)], axis=1)
    conv_b_sample = np.concatenate([R[i]["sso"][:, 64:96].reshape(DEPTH, NSB, 2, D) for i in range(NCORES)], axis=1)
    return tuple(np.ascontiguousarray(a.astype(np.float32)) for a in
                 (y_prompt, y_sample, h_prompt, conv_a_prompt, conv_b_prompt, h_sample, conv_a_sample, conv_b_sample))
```
